# Optimizing a Trainium2 kernel written in Bass

```python
import math
import jax, jax.numpy as jnp
from jax import lax
import numpy as np

D_MODEL = 2048
BATCH = 8
SEQ = 2048
DEPTH = 1

D_ATT = D_MODEL // 2
ATT_HEAD_DIM = 128
N_ATT_HEADS = D_ATT // ATT_HEAD_DIM
Q_BLOCK = 128
D_SSD = D_MODEL
SSD_HEAD_DIM = 64
N_SSD_HEADS = D_SSD // SSD_HEAD_DIM
SSD_GROUPS = 8
D_STATE = 128
CONV_K = 4
CHUNK = 128
D_XBC = D_SSD + 2 * SSD_GROUPS * D_STATE
D_MIX = D_ATT + D_SSD
PROJ_SIZES = (D_ATT, D_ATT, D_ATT, N_ATT_HEADS, D_ATT, D_SSD, D_XBC, N_SSD_HEADS)
D_PROJ = sum(PROJ_SIZES)
NORM_EPS = 1e-5

kernel_name = "fox_ssd_parallel_heads_deepnorm"


def _split_points(sizes):
    pts, acc = [], 0
    for s in sizes[:-1]:
        acc += s
        pts.append(acc)
    return pts


def layer_norm(v, g, b):
    vf = v.astype(jnp.float32)
    mu = jnp.mean(vf, axis=-1, keepdims=True)
    var = jnp.mean(jnp.square(vf - mu), axis=-1, keepdims=True)
    out = (vf - mu) * lax.rsqrt(var + NORM_EPS) * g.astype(jnp.float32) + b.astype(jnp.float32)
    return out.astype(v.dtype)


def fox_attention(q, k, v, log_f):
    b, l, h, d = q.shape
    nb = l // Q_BLOCK
    scale = 1.0 / math.sqrt(d)
    F = jnp.cumsum(log_f.astype(jnp.float32), axis=1)
    Fk = F.transpose(0, 2, 1)
    kf = k.astype(jnp.float32)
    vf = v.astype(jnp.float32)
    qb = q.astype(jnp.float32).reshape(b, nb, Q_BLOCK, h, d).transpose(1, 0, 2, 3, 4)
    Fq = Fk.reshape(b, h, nb, Q_BLOCK).transpose(2, 0, 1, 3)
    q_pos = jnp.arange(l).reshape(nb, Q_BLOCK)
    k_pos = jnp.arange(l)

    def block(args):
        qi, Fi, pos = args
        s = jnp.einsum('bqhd,bkhd->bhqk', qi, kf) * scale
        s = s + (Fi[..., :, None] - Fk[..., None, :])
        s = jnp.where(pos[:, None] >= k_pos[None, :], s, -jnp.inf)
        p = jax.nn.softmax(s, axis=-1)
        return jnp.einsum('bhqk,bkhd->bqhd', p, vf)

    out = lax.map(block, (qb, Fq, q_pos))
    return out.transpose(1, 0, 2, 3, 4).reshape(b, l, h * d)


def causal_depthwise_conv(u, w, bias):
    c = u.shape[-1]
    out = lax.conv_general_dilated(
        u, w[:, None, :].astype(u.dtype), window_strides=(1,), padding=[(CONV_K - 1, 0)],
        dimension_numbers=('NWC', 'WIO', 'NWC'), feature_group_count=c)
    return out + bias


def ssd_chunked(xs, dt, A, Bm, Cm):
    b, l, H, P = xs.shape
    G, N = Bm.shape[2], Bm.shape[3]
    r = H // G
    c = l // CHUNK
    X = xs.astype(jnp.float32).reshape(b, c, CHUNK, G, r, P)
    dtc = dt.reshape(b, c, CHUNK, G, r)
    Bc = Bm.astype(jnp.float32).reshape(b, c, CHUNK, G, N)
    Cc = Cm.astype(jnp.float32).reshape(b, c, CHUNK, G, N)
    a_cs = jnp.cumsum(dtc * A.reshape(G, r), axis=2)
    Xdt = X * dtc[..., None]
    a_t = a_cs.transpose(0, 1, 3, 4, 2)
    seg = a_t[..., :, None] - a_t[..., None, :]
    causal = jnp.tril(jnp.ones((CHUNK, CHUNK), dtype=bool))
    decay = jnp.exp(jnp.where(causal, seg, -jnp.inf))
    cb = jnp.einsum('bclgn,bcsgn->bcgls', Cc, Bc)
    m = cb[:, :, :, None] * decay
    y_diag = jnp.einsum('bcgrls,bcsgrp->bclgrp', m, Xdt)
    decay_end = jnp.exp(a_cs[:, :, -1:] - a_cs)
    states = jnp.einsum('bclgn,bclgr,bclgrp->bcgrpn', Bc, decay_end, Xdt)
    chunk_decay = jnp.exp(a_cs[:, :, -1])

    def step(carry, inp):
        st, dec = inp
        return carry * dec[..., None, None] + st, carry

    init = jnp.zeros((b, G, r, P, N), jnp.float32)
    _, prev = lax.scan(step, init, (states.transpose(1, 0, 2, 3, 4, 5), chunk_decay.transpose(1, 0, 2, 3)))
    prev = prev.transpose(1, 0, 2, 3, 4, 5)
    y_off = jnp.einsum('bclgn,bcgrpn,bclgr->bclgrp', Cc, prev, jnp.exp(a_cs))
    return (y_diag + y_off).reshape(b, l, H, P)


def gated_group_rmsnorm(y, z, w):
    b, l, d = y.shape
    u = (y.astype(jnp.float32) * jax.nn.silu(z.astype(jnp.float32))).reshape(b, l, SSD_GROUPS, d // SSD_GROUPS)
    u = u * lax.rsqrt(jnp.mean(jnp.square(u), axis=-1, keepdims=True) + NORM_EPS)
    return u.reshape(b, l, d) * w.astype(jnp.float32)


def hybrid_mixer(x, w_in, b_forget, conv_w, conv_b, dt_bias, a_log, d_skip, ssd_norm_w, w_out):
    b, l, _ = x.shape
    proj = x @ w_in
    q, k, v, f_logit, z_att, z_ssd, xbc, dt_raw = jnp.split(proj, _split_points(PROJ_SIZES), axis=-1)
    log_f = jax.nn.log_sigmoid(f_logit.astype(jnp.float32) + b_forget.astype(jnp.float32))
    att = fox_attention(q.reshape(b, l, N_ATT_HEADS, ATT_HEAD_DIM),
                        k.reshape(b, l, N_ATT_HEADS, ATT_HEAD_DIM),
                        v.reshape(b, l, N_ATT_HEADS, ATT_HEAD_DIM), log_f)
    att = att * jax.nn.silu(z_att.astype(jnp.float32))
    xbc = jax.nn.silu(causal_depthwise_conv(xbc, conv_w, conv_b))
    xs, Bm, Cm = jnp.split(xbc, [D_SSD, D_SSD + SSD_GROUPS * D_STATE], axis=-1)
    xs = xs.reshape(b, l, N_SSD_HEADS, SSD_HEAD_DIM)
    dt = jax.nn.softplus(dt_raw.astype(jnp.float32) + dt_bias.astype(jnp.float32))
    A = -jnp.exp(a_log.astype(jnp.float32))
    y = ssd_chunked(xs, dt, A, Bm.reshape(b, l, SSD_GROUPS, D_STATE), Cm.reshape(b, l, SSD_GROUPS, D_STATE))
    y = y + d_skip.astype(jnp.float32)[:, None] * xs.astype(jnp.float32)
    y = gated_group_rmsnorm(y.reshape(b, l, D_SSD), z_ssd, ssd_norm_w)
    mixed = jnp.concatenate([att, y], axis=-1).astype(x.dtype)
    return mixed @ w_out


def setup_inputs(seed: int = 0) -> dict:
    key = jax.random.key(seed)
    ks = jax.random.split(key, 12)
    beta = (8.0 * DEPTH) ** -0.25
    x = jax.random.normal(ks[0], (BATCH, SEQ, D_MODEL), jnp.float32)
    w_in = jax.random.normal(ks[1], (DEPTH, D_MODEL, D_PROJ), jnp.float32) * D_MODEL ** -0.5
    b_forget = 2.0 + 0.5 * jax.random.normal(ks[2], (DEPTH, N_ATT_HEADS), jnp.float32)
    conv_w = jax.random.normal(ks[3], (DEPTH, CONV_K, D_XBC), jnp.float32) * CONV_K ** -0.5
    conv_b = 0.02 * jax.random.normal(ks[4], (DEPTH, D_XBC), jnp.float32)
    dt0 = jnp.exp(jax.random.uniform(ks[5], (DEPTH, N_SSD_HEADS), jnp.float32,
                                     math.log(1e-3), math.log(1e-1)))
    dt_bias = dt0 + jnp.log(-jnp.expm1(-dt0))
    a_log = jnp.log(jax.random.uniform(ks[6], (DEPTH, N_SSD_HEADS), jnp.float32, 1.0, 16.0))
    d_skip = 1.0 + 0.1 * jax.random.normal(ks[7], (DEPTH, N_SSD_HEADS), jnp.float32)
    ssd_norm_w = 1.0 + 0.1 * jax.random.normal(ks[8], (DEPTH, D_SSD), jnp.float32)
    w_out = jax.random.normal(ks[9], (DEPTH, D_MIX, D_MODEL), jnp.float32) * (D_MIX ** -0.5) * beta
    ln_g = 1.0 + 0.1 * jax.random.normal(ks[10], (DEPTH, D_MODEL), jnp.float32)
    ln_b = 0.02 * jax.random.normal(ks[11], (DEPTH, D_MODEL), jnp.float32)
    return {"x": x, "w_in": w_in, "b_forget": b_forget, "conv_w": conv_w, "conv_b": conv_b,
            "dt_bias": dt_bias, "a_log": a_log, "d_skip": d_skip, "ssd_norm_w": ssd_norm_w,
            "w_out": w_out, "ln_g": ln_g, "ln_b": ln_b}


def reference(x, w_in, b_forget, conv_w, conv_b, dt_bias, a_log, d_skip, ssd_norm_w, w_out, ln_g, ln_b):
    alpha = (2.0 * DEPTH) ** 0.25
    for i in range(DEPTH):
        h = hybrid_mixer(x, w_in[i], b_forget[i], conv_w[i], conv_b[i], dt_bias[i], a_log[i],
                         d_skip[i], ssd_norm_w[i], w_out[i])
        x = layer_norm(alpha * x + h.astype(x.dtype), ln_g[i], ln_b[i])
    return x
```

```python
import math
import numpy as np
import ml_dtypes
import concourse.bass as bass
import concourse.mybir as mybir
from concourse.bass_utils import run_bass_kernel_spmd

F32 = mybir.dt.float32
BF16 = mybir.dt.bfloat16
AF = mybir.ActivationFunctionType
ALU = mybir.AluOpType

L = 2048
D = 2048
NT = 16
NCORES = 8
SCALE = 1.0 / math.sqrt(128.0)
NEG = -30000.0
ALPHA = 2.0 ** 0.25
EPS = 1e-5

ENGS = ("pe", "act", "dve", "pool", "sp")


class Buf:
    __slots__ = ("name", "w", "r", "excl")

    def __init__(self, name, excl=False):
        self.name = name
        self.w = None
        self.r = {}
        self.excl = excl


class Prog:
    def __init__(self, nc):
        self.nc = nc
        self.ops = {e: [] for e in ENGS}
        self.cnt = {}
        self.known = {e: {} for e in ENGS}
        self.snap = {}
        self.sem_order = []
        self.nwaits = 0

    def _sem(self, key):
        if key not in self.cnt:
            self.cnt[key] = 0
            self.sem_order.append(key)

    def _collect(self, eng, reads, writes):
        waits = {}
        known = self.known[eng]

        def need(dep):
            if dep is None:
                return
            k, v = dep
            if eng == "pe" and k == "pe":
                return
            if known.get(k, 0) >= v:
                return
            if waits.get(k, 0) < v:
                waits[k] = v

        for b in reads:
            need(b.w)
        for b in writes:
            need(b.w)
            for d in b.r.items():
                need(d)
        for k, v in list(waits.items()):
            if known.get(k, 0) < v:
                known[k] = v
            s = self.snap.get((k, v))
            if s:
                for k2, v2 in s.items():
                    if known.get(k2, 0) < v2:
                        known[k2] = v2
        return {k: v for k, v in waits.items() if known.get(k, 0) <= v}

    def _mark(self, tok, reads, writes):
        for b in reads:
            if b.r.get(tok[0], 0) < tok[1]:
                b.r[tok[0]] = tok[1]
        for b in writes:
            b.w = tok
            b.r = {}

    def op(self, eng, fn, reads=(), writes=(), signal=True):
        self._sem(eng)
        ex = [b for b in reads if b.excl and b not in writes]
        if ex:
            writes = list(writes) + ex
        waits = self._collect(eng, reads, writes)
        if signal:
            self.cnt[eng] += 1
            tok = (eng, self.cnt[eng])
            self.snap[tok] = dict(self.known[eng])
        else:
            tok = (eng, self.cnt[eng] + 1)
        self._mark(tok, reads, writes)
        self.nwaits += len(waits)
        self.ops[eng].append((waits, fn, (eng, 1) if signal else None))
        return tok

    def dma(self, eng, pairs, key, reads=(), writes=(), **kw):
        key = "d:" + key
        self._sem(key)
        waits = self._collect(eng, reads, writes)
        for i, (o, a) in enumerate(pairs):
            def fn(e, o=o, a=a):
                return e.dma_start(out=o, in_=a, **kw)
            self.ops[eng].append((waits if i == 0 else {}, fn, (key, 16)))
        self.cnt[key] += 16 * len(pairs)
        tok = (key, self.cnt[key])
        self.snap[tok] = dict(self.known[eng])
        self._mark(tok, reads, writes)
        self.nwaits += len(waits)
        return tok

    def wait_all(self, eng, toks):
        waits = {}
        for k, v in toks:
            if self.known[eng].get(k, 0) < v and waits.get(k, 0) < v:
                waits[k] = v
        for k, v in waits.items():
            self.known[eng][k] = v
        self.ops[eng].append((waits, None, None))

    def barrier(self):
        toks = [(k, v) for k, v in self.cnt.items() if v > 0]
        for e in ENGS:
            self.wait_all(e, toks)

    def emit(self):
        nc = self.nc
        from contextlib import ExitStack
        with ExitStack() as st:
            sems = {}
            for k in self.sem_order:
                sems[k] = st.enter_context(nc.semaphore("s%d" % len(sems)))
            block = st.enter_context(nc.Block())

            def run(ename):
                def body(e):
                    for waits, fn, inc in self.ops[ename]:
                        for k, v in waits.items():
                            e.wait_ge(sems[k], v)
                        if fn is None:
                            continue
                        ins = fn(e)
                        if inc is not None:
                            ins.then_inc(sems[inc[0]], inc[1])
                return body

            block.tensor(run("pe"))
            block.scalar(run("act"))
            block.vector(run("dve"))
            block.gpsimd(run("pool"))
            block.sync(run("sp"))


class Mem:
    def __init__(self, nc, base, top):
        self.nc = nc
        self.base = base
        self.top = top
        self.cur = base
        self.n = 0
        self.peak = base

    def alloc(self, shape, dtype, name="t"):
        esz = 2 if dtype == BF16 else 4
        nbytes = int(np.prod(shape[1:])) * esz
        off = (self.cur + 31) // 32 * 32
        assert off + nbytes <= self.top, "SBUF overflow %s need %d at %d top %d" % (name, nbytes, off, self.top)
        self.cur = off + nbytes
        self.peak = max(self.peak, self.cur)
        self.n += 1
        return self.nc.alloc_sbuf_tensor_at("%s_%d" % (name, self.n), list(shape), dtype, offset=off)

    def mark(self):
        return self.cur

    def release(self, mark):
        self.cur = mark


Q0, K0, V0, F0, ZA0, ZS0, XBC0, DT0 = 0, 1024, 2048, 3072, 3080, 4104, 6152, 10248


def slab_plan():
    plan = []
    f = [-1] * 128
    dtc = [-1] * 128
    for r in (0, 32, 64):
        for i in range(8):
            f[r + i] = F0 + i
        for i in range(32):
            dtc[r + i] = DT0 + i
    plan.append(("f", f))
    plan.append(("dt", dtc))
    for hp in range(4):
        plan.append(("v%d" % hp, list(range(V0 + hp * 256, V0 + (hp + 1) * 256))))
        for h in (2 * hp, 2 * hp + 1):
            plan.append(("q%d" % h, list(range(Q0 + h * 128, Q0 + (h + 1) * 128))))
            plan.append(("k%d" % h, list(range(K0 + h * 128, K0 + (h + 1) * 128))))
            plan.append(("za%d" % h, list(range(ZA0 + h * 128, ZA0 + (h + 1) * 128))))
    for g in range(8):
        plan.append(("z%d" % g, list(range(ZS0 + g * 256, ZS0 + (g + 1) * 256))))
        plan.append(("xs%d" % g, list(range(XBC0 + g * 256, XBC0 + (g + 1) * 256))))
        plan.append(("B%d" % g, list(range(XBC0 + 2048 + g * 128, XBC0 + 2048 + (g + 1) * 128))))
        plan.append(("C%d" % g, list(range(XBC0 + 3072 + g * 128, XBC0 + 3072 + (g + 1) * 128))))
    offs = {}
    o = 0
    for name, cols in plan:
        offs[name] = (o, len(cols))
        o += 16 * len(cols)
    return plan, offs, o


CP_FIELDS = [("ident", 128), ("mask4", 512), ("ones", 512), ("selAneg", 1024), ("selS", 4096), ("selSneg", 4096),
             ("prm3", 3), ("dskip", 32), ("normw", 2048), ("lng", 2048),
             ("lnb", 2048), ("convw", 128), ("convbc", 32), ("convbr", 4096)]


def cp_layout():
    offs = {}
    o = 0
    for n, w in CP_FIELDS:
        offs[n] = (o, w)
        o += w
    return offs, o


def make_wpack(w_in):
    plan, offs, tot = slab_plan()
    wp = np.zeros((128, tot), np.float32)
    for name, cols in plan:
        o, W = offs[name]
        blk = np.zeros((D, W), np.float32)
        idx = np.array(cols)
        ok = idx >= 0
        blk[:, ok] = w_in[:, idx[ok]]
        wp[:, o:o + 16 * W] = blk.reshape(16, 128, W).transpose(1, 0, 2).reshape(128, 16 * W)
    return wp


def make_cpack(b_forget, conv_w, conv_b, dt_bias, a_log, d_skip, ssd_norm_w, ln_g, ln_b):
    offs, tot = cp_layout()
    cp = np.zeros((128, tot), np.float32)

    def put(name, arr):
        o, w = offs[name]
        cp[:, o:o + w] = arr

    put("ident", np.eye(128, dtype=np.float32))
    s = np.arange(128)[:, None]
    l = np.arange(128)[None, :]
    maskT = np.where(l >= s, 0.0, NEG).astype(np.float32)
    put("mask4", np.tile(maskT, (1, 4)))
    put("ones", np.ones((128, 512), np.float32))
    selA = np.zeros((128, 8, 128), np.float32)
    selS = np.zeros((128, 32, 128), np.float32)
    for k in range(96):
        selS[k, k % 32, :] = 1.0
        if k < 72 and (k % 32) < 8:
            selA[k, k % 32, :] = -1.0
    put("selAneg", selA.reshape(128, -1))
    put("selS", selS.reshape(128, -1))
    put("selSneg", -selS.reshape(128, -1))
    bf = np.zeros((128, 1), np.float32)
    dtb = np.zeros((128, 1), np.float32)
    alog = np.zeros((128, 1), np.float32)
    for r in (0, 32, 64):
        bf[r:r + 8, 0] = b_forget
        dtb[r:r + 32, 0] = dt_bias
        alog[r:r + 32, 0] = a_log
    put("prm3", np.concatenate([bf, dtb, alog], axis=1))
    put("dskip", np.tile(d_skip[None, :], (128, 1)))
    put("normw", np.tile(ssd_norm_w[None, :], (128, 1)))
    put("lng", np.tile(ln_g[None, :], (128, 1)))
    put("lnb", np.tile(ln_b[None, :], (128, 1)))
    put("convw", conv_w.T.reshape(32, 128, 4).transpose(1, 0, 2).reshape(128, 128))
    put("convbc", conv_b.reshape(32, 128).T)
    cbr = np.zeros((128, 4096), np.float32)
    cbr[0] = conv_b
    cbr[32] = conv_b
    put("convbr", cbr)
    return cp


def build_program(dbg=False, phases=3):
    nc = bass.Bass("TRN2", target_bir_lowering=False)
    plan, woffs, wtot = slab_plan()
    coffs, ctot = cp_layout()

    x_tok = nc.dram_tensor("x_tok", [L, D], F32, kind="ExternalInput").ap()
    xT = nc.dram_tensor("xT", [D, L], F32, kind="ExternalInput").ap()
    wpack = nc.dram_tensor("wpack", [128, wtot], F32, kind="ExternalInput").ap()
    w_out = nc.dram_tensor("w_out", [3072, D], F32, kind="ExternalInput").ap()
    cpack = nc.dram_tensor("cpack", [128, ctot], F32, kind="ExternalInput").ap()
    out = nc.dram_tensor("out", [L, D], F32, kind="ExternalOutput").ap()
    mixs = nc.dram_tensor("mixs", [4, 128, 24, 512], BF16, kind="ExternalOutput" if dbg else "Internal").ap()

    wob = nc.dram_tensor("wob", [128, 24, D], BF16, kind="Internal").ap()
    b_wob = [Buf("wob%d" % i) for i in range(24)]
    p = Prog(nc)
    mem = Mem(nc, nc.sbuf_base, nc.sbuf_top)
    banks = [nc.alloc_psum_tensor("bank%d" % i, [128, 512], F32) for i in range(8)]
    bkb = [Buf("bank%d" % i, excl=True) for i in range(8)]

    def cp(name, a=0, b=None):
        o, w = coffs[name]
        if b is None:
            b = w
        return cpack[:, o + a:o + b]

    def MM(outap, lhsT, rhs, start, stop, reads, writes, signal=None):
        if signal is None:
            signal = stop
        return p.op("pe", lambda e: e.matmul(outap, lhsT, rhs, start=start, stop=stop), reads, writes, signal)

    def TR(outap, in_, ident, reads, writes, signal=True):
        return p.op("pe", lambda e: e.transpose(outap, in_, ident), reads, writes, signal)

    def ACT(outap, in_, func, reads, writes, bias=None, scale=1.0, accum=None):
        def fn(e):
            kw = {}
            if bias is not None:
                kw["bias"] = bias
            if accum is not None:
                kw["accum_out"] = accum
            return e.activation(out=outap, in_=in_, func=func, scale=scale, **kw)
        return p.op("act", fn, reads, writes)

    def TT(eng, outap, in0, in1, op, reads, writes):
        return p.op(eng, lambda e: e.tensor_tensor(out=outap, in0=in0, in1=in1, op=op), reads, writes)

    def TS(eng, outap, in0, s1, s2, op0, op1, reads, writes):
        def fn(e):
            if s2 is None:
                return e.tensor_scalar(out=outap, in0=in0, scalar1=s1, scalar2=None, op0=op0)
            return e.tensor_scalar(out=outap, in0=in0, scalar1=s1, scalar2=s2, op0=op0, op1=op1)
        return p.op(eng, fn, reads, writes)

    def STT(eng, outap, in0, scalar, in1, op0, op1, reads, writes):
        return p.op(eng, lambda e: e.scalar_tensor_tensor(out=outap, in0=in0, scalar=scalar, in1=in1,
                                                          op0=op0, op1=op1), reads, writes)

    def CPY(eng, outap, in_, reads, writes):
        if eng == "act":
            return p.op("act", lambda e: e.copy(out=outap, in_=in_), reads, writes)
        return p.op(eng, lambda e: e.tensor_copy(out=outap, in_=in_), reads, writes)

    b_xT = [Buf("xT%d" % i) for i in range(16)]
    xT_bf = mem.alloc([128, 16, L], BF16, "xT_bf")
    w_bf = nc.alloc_sbuf_tensor_at("w_out_bf", [128, 24, D], BF16, offset=(mem.base + 31) // 32 * 32)
    b_w = [Buf("w%d" % i) for i in range(24)]
    w_v = w_out.rearrange("(kb p) n -> p kb n", p=128)
    ident_bf = mem.alloc([128, 128], BF16, "ident_bf")
    ident_f = mem.alloc([128, 128], F32, "ident_f")
    mask4_bf = mem.alloc([128, 512], BF16, "mask4")
    ones_bf = mem.alloc([128, 512], BF16, "ones_bf")
    ones_f = mem.alloc([128, 128], F32, "ones_f")
    prm = mem.alloc([128, 8], F32, "prm")
    dskip = mem.alloc([128, 32], F32, "dskip")
    convw = mem.alloc([128, 128], F32, "convw")
    convbc = mem.alloc([128, 32], F32, "convbc")
    b_const = Buf("const")
    b_prm = Buf("prm")
    slabs = [mem.alloc([128, 16, 256], BF16, "slab%d" % i) for i in range(2)]
    b_slab = [Buf("slab%d" % i) for i in range(2)]
    Ast = mem.alloc([128, L], BF16, "Ast")
    dt_tok = mem.alloc([128, 16, 32], F32, "dt_tok")
    ea_tok = mem.alloc([128, 16, 32], F32, "ea_tok")
    cd_bc = mem.alloc([128, 16, 32], F32, "cd_bc")
    dtde = mem.alloc([128, 16, 32], F32, "dtde")
    b_ssdc = Buf("ssdc")
    ttb = [mem.alloc([128, 512], F32, "tt%d" % i) for i in range(2)]
    b_tt = [Buf("tt%d" % i) for i in range(2)]
    tti = [0]

    def SILU2(outap, psum_ap, reads, writes, in_view=None, tt_view=None):
        k = tti[0] % 2
        tti[0] += 1
        ACT(ttb[k][:, :], psum_ap, AF.Tanh, reads, [b_tt[k]], scale=0.5)
        tv = ttb[k][:, :] if tt_view is None else tt_view(ttb[k][:, :])
        pv = psum_ap if in_view is None else in_view(psum_ap)
        STT("dve", outap, tv, 1.0, pv, ALU.add, ALU.mult, [b_tt[k]] + list(reads), writes)

    p.dma("pool", [(ident_bf[:, :], cp("ident")), (mask4_bf[:, :], cp("mask4")), (ones_bf[:, :], cp("ones")),
                   ], "constbf", writes=[b_const])
    p.dma("sp", [(ident_f[:, :], cp("ident")), (ones_f[:, :], cp("ones", 0, 128)), (prm[:, 0:3], cp("prm3")),
                 (dskip[:, :], cp("dskip")),
                 (convw[:, :], cp("convw")), (convbc[:, :], cp("convbc"))], "constf", writes=[b_const, b_prm])
    slab_i = [0]

    def load_slab(name):
        i = slab_i[0] % 2
        slab_i[0] += 1
        o, W = woffs[name]
        dst = slabs[i][:, :, 0:W] if W == 256 else slabs[i][:, :, 0:W]
        src = wpack[:, o:o + 16 * W].rearrange("p (kc w) -> p kc w", w=W)
        p.dma("pool", [(dst, src)], "slab%d" % i, writes=[b_slab[i]])
        return slabs[i], b_slab[i]

    slab_f, bs_f = load_slab("f")
    slab_dt, bs_dt = load_slab("dt")
    xTv = xT.rearrange("(kc p) l -> p kc l", p=128)
    for kc in range(16):
        p.dma("pool", [(xT_bf[:, kc, :], xTv[:, kc, :])], "xT%d" % kc, writes=[b_xT[kc]])

    def proj_fm(slab, bslab, j0, bank_ids, evac):
        for g4 in range(4):
            bk = bank_ids[g4 % len(bank_ids)]
            for kc in range(16):
                MM(banks[bk][:, :], slab[:, kc, j0:j0 + 128], xT_bf[:, kc, g4 * 512:(g4 + 1) * 512],
                   kc == 0, kc == 15, [bslab, b_xT[kc]], [bkb[bk]])
            evac(g4, bk)

    m0 = mem.mark()
    Fst = mem.alloc([128, L], BF16, "Fst")
    selA_bf = mem.alloc([128, 8, 128], BF16, "selA")
    G_tok = mem.alloc([128, 16, 8], F32, "G_tok")
    qT = [mem.alloc([128, L], BF16, "qT0")]
    kT = [mem.alloc([128, L], BF16, "kT0")]
    gz = [mem.alloc([128, L], F32, "gz0")]
    vp = [mem.alloc([128, 16, 256], BF16, "vp0")]
    b_q = [Buf("q%d" % i) for i in range(2)]; b_k = [Buf("k%d" % i) for i in range(2)]
    b_gz = [Buf("gz%d" % i) for i in range(2)]; b_vp = [Buf("vp%d" % i) for i in range(2)]
    p1_slabs = [mem.alloc([128, 16, 256], BF16, "xslab%d" % i) for i in range(2)] + slabs
    p1_bufs = [Buf("slab%d" % i) for i in (2, 3)] + b_slab
    p1_keys = ["slab2", "slab3", "slab0", "slab1"]
    p1_order = []
    for hp_ in range(4):
        p1_order.append("v%d" % hp_)
        for h_ in (2 * hp_, 2 * hp_ + 1):
            p1_order += ["q%d" % h_, "k%d" % h_, "za%d" % h_]
    p1_issued = {}
    p1_state = [0, 0]

    def p1_fill():
        while p1_state[0] < len(p1_order) and p1_state[0] - p1_state[1] < 4:
            i = p1_state[0]
            nm = p1_order[i]
            o, W = woffs[nm]
            src = wpack[:, o:o + 16 * W].rearrange("p (kc w) -> p kc w", w=W)
            p.dma("pool", [(p1_slabs[i % 4][:, :, 0:W], src)], p1_keys[i % 4], writes=[p1_bufs[i % 4]])
            p1_issued[nm] = (p1_slabs[i % 4], p1_bufs[i % 4])
            p1_state[0] += 1

    def p1_get(nm):
        assert p1_order[p1_state[1]] == nm
        if nm not in p1_issued:
            p1_fill()
        p1_state[1] += 1
        return p1_issued[nm]

    def proj_v(hp):
        slab, bs = p1_get("v%d" % hp)
        vs = hp % 2
        for t2 in range(8):
            bk = t2 % 2
            for tt in range(2):
                t = t2 * 2 + tt
                for kc in range(16):
                    MM(banks[bk][:, tt * 256:(tt + 1) * 256], xT_bf[:, kc, t * 128:(t + 1) * 128],
                       slab[:, kc, 0:256], kc == 0, kc == 15, [bs, b_xT[kc]], [bkb[bk]],
                       signal=(kc == 15 and tt == 1))
            eng = "act" if (t2 % 2 == 0 or hp == 0) else "dve"
            CPY(eng, vp[vs][:, 2 * t2:2 * t2 + 2, :].rearrange("p a b -> p (a b)"), banks[bk][:, :],
                [bkb[bk]], [b_vp[vs]])

    def proj_q(h):
        hs = h % 2
        slab, bs = p1_get("q%d" % h)
        proj_fm(slab, bs, 0, [0, 1], lambda g4, bk: ACT(
            qT[hs][:, g4 * 512:(g4 + 1) * 512], banks[bk][:, :], AF.Copy, [bkb[bk]], [b_q[hs]], scale=SCALE))

    def proj_k(h):
        hs = h % 2
        slab, bs = p1_get("k%d" % h)
        proj_fm(slab, bs, 0, [0, 1], lambda g4, bk: CPY(
            "dve", kT[hs][:, g4 * 512:(g4 + 1) * 512], banks[bk][:, :], [bkb[bk]], [b_k[hs]]))

    def proj_za(h):
        hs = h % 2
        slab, bs = p1_get("za%d" % h)
        proj_fm(slab, bs, 0, [0, 1], lambda g4, bk: SILU2(
            gz[hs][:, g4 * 512:(g4 + 1) * 512], banks[bk][:, :], [bkb[bk]], [b_gz[hs]]))

    m1 = mem.mark()
    Gf = mem.alloc([128, L], F32, "Gf")
    dtA = mem.alloc([128, L], F32, "dtA")
    r1 = mem.alloc([128, L], F32, "r1")
    tmpb = mem.alloc([128, L], BF16, "tmpb")
    b_G = Buf("G"); b_r1 = Buf("r1"); b_tmpb = Buf("tmpb")
    b_F = Buf("Fst"); b_Gtok = Buf("Gtok")
    p.dma("pool", [(selA_bf[:, :, :].rearrange("p a b -> p (a b)"), cp("selAneg"))], "selA", writes=[b_const])

    TS("dve", prm[:, 3:4], prm[:, 0:1], -1.0, None, ALU.mult, None, [b_const], [b_prm])
    ACT(prm[:, 5:6], prm[:, 2:3], AF.Exp, [b_prm], [b_prm])
    TS("dve", prm[:, 4:5], prm[:, 5:6], -1.0, None, ALU.mult, None, [b_prm], [b_prm])

    def split3(src, bsrc, dst, bdst):
        CPY("dve", dst[:, :], src[:, :], [bsrc], [bdst])
        TT("dve", r1[:, :], src[:, :], dst[:, :], ALU.subtract, [bsrc, bdst], [b_r1])
        CPY("dve", dst[32:64, :], r1[32:64, :], [b_r1], [bdst])
        CPY("dve", tmpb[64:96, :], r1[64:96, :], [b_r1], [b_tmpb])
        TT("dve", r1[64:96, :], r1[64:96, :], tmpb[64:96, :], ALU.subtract, [b_r1, b_tmpb], [b_r1])
        CPY("dve", dst[64:96, :], r1[64:96, :], [b_r1], [bdst])

    dtT = mem.alloc([128, L], F32, "dtT")
    acsT = mem.alloc([128, L], F32, "acsT")
    a_tok = mem.alloc([128, 16, 32], F32, "a_tok")
    dtA_tok = mem.alloc([128, 16, 32], F32, "dtA_tok")
    b_dtT = Buf("dtT"); b_acs = Buf("acs"); b_dtA = Buf("dtA"); b_atok = Buf("atok"); b_Ast = Buf("Ast")
    fl = lambda t: t[:, :, :].rearrange("p a b -> p (a b)")
    for kc in range(16):
        for u in range(8):
            sl_, bs_ = (slab_f, bs_f) if u < 4 else (slab_dt, bs_dt)
            g4 = u % 4
            MM(banks[u][:, :], sl_[:, kc, 0:128], xT_bf[:, kc, g4 * 512:(g4 + 1) * 512],
               kc == 0, kc == 15, [bs_, b_xT[kc]], [bkb[u]])
    for g4 in range(4):
        ACT(Gf[:, g4 * 512:(g4 + 1) * 512], banks[g4][:, :], AF.Exp, [bkb[g4], b_prm], [b_G],
            bias=prm[:, 3:4], scale=-1.0)
    for g4 in range(4):
        ACT(dtT[:, g4 * 512:(g4 + 1) * 512], banks[4 + g4][:, :], AF.Exp, [bkb[4 + g4], b_prm], [b_dtT],
            bias=prm[:, 1:2])
    p1_fill()
    ACT(Gf[:, :], Gf[:, :], AF.Ln, [b_G], [b_G], bias=1.0)
    p.op("dve", lambda e: e.tensor_tensor_scan(out=Gf[:, :], data0=ones_f[:, 0:1].to_broadcast([128, L]),
                                               data1=Gf[:, :], initial=0.0, op0=ALU.mult, op1=ALU.add),
         [b_G, b_const], [b_G])
    split3(Gf, b_G, Fst, b_F)
    ACT(dtT[:, :], dtT[:, :], AF.Ln, [b_dtT], [b_dtT], bias=1.0)
    TS("dve", dtA[:, :], dtT[:, :], prm[:, 4:5], None, ALU.mult, None, [b_dtT, b_prm], [b_dtA])
    for c in range(16):
        p.op("dve", lambda e, c=c: e.tensor_tensor_scan(
            out=acsT[:, c * 128:(c + 1) * 128], data0=ones_f[:, 0:1].to_broadcast([128, 128]),
            data1=dtA[:, c * 128:(c + 1) * 128], initial=0.0, op0=ALU.mult, op1=ALU.add), [b_dtA, b_const], [b_acs])
    split3(acsT, b_acs, Ast, b_Ast)
    if phases >= 1:
        proj_v(0)
        proj_q(0)
    for t in range(16):
        TR(banks[2][:, t * 8:(t + 1) * 8], Gf[0:8, t * 128:(t + 1) * 128], ident_f[0:8, 0:8],
           [b_G, b_const], [bkb[2]], signal=(t == 15))
    for (src, bsrc, bk) in ((dtT, b_dtT, 3), (acsT, b_acs, 4), (dtA, b_dtA, 5)):
        for c in range(16):
            TR(banks[bk][:, c * 32:(c + 1) * 32], src[0:32, c * 128:(c + 1) * 128], ident_f[0:32, 0:32],
               [bsrc, b_const], [bkb[bk]], signal=(c == 15))
    CPY("dve", G_tok[:, :, :].rearrange("p a b -> p (a b)"), banks[2][:, 0:128], [bkb[2]], [b_Gtok])
    CPY("dve", fl(dt_tok), banks[3][:, :], [bkb[3]], [b_ssdc])
    CPY("dve", fl(a_tok), banks[4][:, :], [bkb[4]], [b_atok])
    CPY("dve", fl(dtA_tok), banks[5][:, :], [bkb[5]], [b_atok])
    for c in range(16):
        MM(banks[6][:, c * 32:(c + 1) * 32], ones_f[:, :], dtA_tok[:, c, :], c == 0, c == 15,
           [b_atok, b_const], [bkb[6]])
    ACT(fl(cd_bc), banks[6][:, :], AF.Exp, [bkb[6]], [b_ssdc])
    ACT(fl(ea_tok), fl(a_tok), AF.Exp, [b_atok], [b_ssdc])
    TT("dve", fl(a_tok), banks[6][:, :], fl(a_tok), ALU.subtract, [bkb[6], b_atok], [b_atok])
    ACT(fl(dtde), fl(a_tok), AF.Exp, [b_atok], [b_ssdc])
    TT("dve", fl(dtde), fl(dtde), fl(dt_tok), ALU.mult, [b_ssdc], [b_ssdc])
    for tbl in (dt_tok, dtde, ea_tok):
        TS("dve", fl(tbl), fl(tbl), 0.5, None, ALU.mult, None, [b_ssdc], [b_ssdc])
    TS("dve", dskip[:, :], dskip[:, :], 0.5, None, ALU.mult, None, [b_const], [b_const])
    if phases >= 1:
        proj_k(0)
        proj_za(0)
    p.barrier()
    mem.release(m1)

    mix_toks = []
    zslab = {}
    if phases >= 1:
        qT.append(mem.alloc([128, L], BF16, "qT1"))
        kT.append(mem.alloc([128, L], BF16, "kT1"))
        gz.append(mem.alloc([128, L], F32, "gz1"))
        vp.append(mem.alloc([128, 16, 256], BF16, "vp1"))
        Pb = [mem.alloc([128, 512], BF16, "P%d" % i) for i in range(3)]
        rinv = [mem.alloc([128, 512], F32, "rinv%d" % i) for i in range(2)]
        tO = [mem.alloc([128, 512], F32, "tO%d" % i) for i in range(2)]
        attst = [mem.alloc([128, L], BF16, "attst%d" % i) for i in range(2)]
        b_P = [Buf("P%d" % i) for i in range(3)]; b_rinv = [Buf("rinv%d" % i) for i in range(2)]
        b_tO = [Buf("tO%d" % i) for i in range(2)]; b_att = [Buf("att%d" % i) for i in range(2)]
        for h in range(8):
            hs = h % 2
            vs = (h // 2) % 2
            hl = h % 2
            if h > 0:
                if h % 2 == 0:
                    proj_v(h // 2)
                proj_q(h)
                proj_k(h)
                proj_za(h)
            if h == 7 and phases >= 2:
                zslab[0] = load_slab("z0")

            p1_fill()
            if phases >= 3:
                for kb in range(3 * h, 3 * h + 3):
                    p.dma("pool", [(wob[:, kb, :], w_v[:, kb, :])], "wob%d" % kb, writes=[b_wob[kb]])
            steps = [(g, kb) for g in range(4) for kb in range(4 * g + 4)]

            def S_mm(i):
                g, kb = steps[i]
                sb = 2 + (i % 2)
                j = kb - 4 * g
                c0 = j * 128 if j > 0 else 0
                n = 512 - c0
                diag = j >= 0
                q0 = g * 512 + c0
                MM(banks[sb][:, c0:512], kT[hs][:, kb * 128:(kb + 1) * 128], qT[hs][:, q0:q0 + n],
                   True, False, [b_k[hs], b_q[hs]], [bkb[sb]], signal=False)
                MM(banks[sb][:, c0:512], selA_bf[:, h, :], Fst[:, q0:q0 + n],
                   False, not diag, [b_const, b_F], [bkb[sb]], signal=not diag)
                if diag:
                    MM(banks[sb][:, c0:c0 + 128], ident_bf[:, :], mask4_bf[:, 0:128],
                       False, True, [b_const], [bkb[sb]], signal=True)

            def E_act(i):
                g, kb = steps[i]
                sb = 2 + (i % 2)
                j = kb - 4 * g
                c0 = j * 128 if j > 0 else 0
                ps = i % 3
                ACT(Pb[ps][:, c0:512], banks[sb][:, c0:512], AF.Exp, [bkb[sb], b_Gtok], [b_P[ps]],
                    bias=G_tok[:, kb, h:h + 1])

            def PV_mm(i):
                g, kb = steps[i]
                j = kb - 4 * g
                c0 = j * 128 if j > 0 else 0
                ps = i % 3
                ob = 4 + (g % 2)
                rb = 6 + (g % 2)
                last = kb == 4 * g + 3
                MM(banks[ob][:, c0:512], vp[vs][:, kb, hl * 128:(hl + 1) * 128], Pb[ps][:, c0:512],
                   kb == 0, last, [b_vp[vs], b_P[ps]], [bkb[ob]])
                MM(banks[rb][:, c0:512], ones_bf[:, 0:128], Pb[ps][:, c0:512],
                   kb == 0, last, [b_const, b_P[ps]], [bkb[rb]])
                if last:
                    gs = g % 2
                    p.op("dve", lambda e: e.reciprocal(out=rinv[gs][:, :], in_=banks[rb][:, :]),
                         [bkb[rb]], [b_rinv[gs]])
                    STT("dve", tO[gs][:, :], banks[ob][:, :], 0.5, rinv[gs][:, :], ALU.mult, ALU.mult,
                       [bkb[ob], b_rinv[gs]], [b_tO[gs]])
                    TT("pool", attst[hs][:, g * 512:(g + 1) * 512], tO[gs][:, :], gz[hs][:, g * 512:(g + 1) * 512],
                       ALU.mult, [b_tO[gs], b_gz[hs]], [b_att[hs]])

            S_mm(0)
            for i in range(len(steps)):
                E_act(i)
                if i + 1 < len(steps):
                    S_mm(i + 1)
                PV_mm(i)
            mix_toks.append(p.dma("sp", [(mixs[q4, :, h, :], attst[hs][:, q4 * 512:(q4 + 1) * 512]) for q4 in range(4)],
                                  "att%d" % hs, reads=[b_att[hs]]))
    p.barrier()
    mem.release(m0)

    if phases >= 2:
        uT = mem.alloc([128, 2, L + 3], BF16, "uT")
        dg = mem.alloc([128, 16, 128], BF16, "dg")
        dI = [mem.alloc([128, 8, 128], BF16, "dI%d" % i) for i in range(2)]
        selp = [mem.alloc([128, 4, 128], BF16, "selp%d" % i) for i in range(2)]
        seln = [mem.alloc([128, 4, 128], BF16, "seln%d" % i) for i in range(2)]
        BT = [mem.alloc([128, L], BF16, "BT%d" % i) for i in range(2)]
        CT = [mem.alloc([128, L], BF16, "CT%d" % i) for i in range(2)]
        B_tok = [mem.alloc([128, 16, 128], BF16, "B_tok%d" % i) for i in range(2)]
        xs_tok = [mem.alloc([128, 16, 256], BF16, "xs_tok%d" % i) for i in range(2)]
        gz_tok = [mem.alloc([128, 16, 256], BF16, "gz_tok%d" % i) for i in range(2)]
        normw = [mem.alloc([128, 256], F32, "normw%d" % i) for i in range(3)]
        cbst = [mem.alloc([128, 512], BF16, "cbst%d" % i) for i in range(2)]
        u_all = mem.alloc([128, 16, 256], BF16, "u_all")
        cbr_f = mem.alloc([64, 512], F32, "cbr_f")
        cbr_t = mem.alloc([64, 512], F32, "cbr_t")
        dhi = mem.alloc([128, 4], BF16, "dhi")
        dlo = mem.alloc([128, 4], F32, "dlo")
        cb_bf = [mem.alloc([128, 128], BF16, "cb_bf%d" % i) for i in range(2)]
        dec_bf = [mem.alloc([128, 512], BF16, "dec%d" % i) for i in range(2)]
        MT_bf = [mem.alloc([128, 512], BF16, "MT%d" % i) for i in range(2)]
        Xdt = [mem.alloc([128, 256], BF16, "Xdt%d" % i) for i in range(2)]
        Xdd = [mem.alloc([128, 256], BF16, "Xdd%d" % i) for i in range(2)]
        t1 = [mem.alloc([128, 256], F32, "t1%d" % i) for i in range(2)]
        t2 = [mem.alloc([128, 256], F32, "t2%d" % i) for i in range(2)]
        carry = mem.alloc([128, 256], F32, "carry")
        prev_bf = mem.alloc([128, 256], BF16, "prev_bf")
        yn_bf = [mem.alloc([128, 256], BF16, "yn%d" % i) for i in range(2)]
        ysm = [mem.alloc([128, 2, 128], BF16, "ysm%d" % i) for i in range(2)]
        ss = mem.alloc([128, 16], F32, "ss")
        rstd = mem.alloc([128, 16], F32, "rstd")
        junk = mem.alloc([128, 256], BF16, "junk")
        b_uT = [Buf("uT%d" % i) for i in range(2)]; b_dg = Buf("dg")
        b_dI = [Buf("dI%d" % i) for i in range(2)]; b_sel = [Buf("sel%d" % i) for i in range(2)]
        b_BT = [Buf("BT%d" % i) for i in range(2)]; b_CT = [Buf("CT%d" % i) for i in range(2)]
        b_Btok = [Buf("Btok%d" % i) for i in range(2)]; b_xs = [Buf("xs%d" % i) for i in range(2)]
        b_gzt = [Buf("gzt%d" % i) for i in range(2)]; b_normw = [Buf("normw%d" % i) for i in range(3)]
        b_cbst = [Buf("cbst%d" % i) for i in range(2)]
        b_uall = [Buf("uall%d" % i) for i in range(16)]; b_tr = [Buf("tr%d" % i) for i in range(2)]; b_cbr = Buf("cbr"); b_dhl = Buf("dhl")
        b_cb = [Buf("cb%d" % i) for i in range(2)]
        b_dec = [Buf("dec%d" % i) for i in range(2)]; b_MT = [Buf("MT%d" % i) for i in range(2)]
        b_Xdt = [Buf("Xdt%d" % i) for i in range(2)]; b_Xdd = [Buf("Xdd%d" % i) for i in range(2)]
        b_t1 = [Buf("t1%d" % i) for i in range(2)]; b_t2 = [Buf("t2%d" % i) for i in range(2)]
        b_carry = Buf("carry"); b_prev = Buf("prev"); b_yn = [Buf("yn%d" % i) for i in range(2)]
        b_ysm = [Buf("ysm%d" % i) for i in range(2)]
        b_ss = Buf("ss"); b_rstd = Buf("rstd"); b_junk = Buf("junk")
        p.op("dve", lambda e: e.memset(uT[:, :, 0:3], 0.0), [], b_uT)
        for i in range(2):
            p.op("dve", lambda e, i=i: e.memset(cbst[i][:, :], 0.0), [], [b_cbst[i]])
        cbr0 = coffs["convbr"][0]
        sp0 = coffs["selS"][0]
        sn0 = coffs["selSneg"][0]

        def bc4(ap2d):
            return ap2d.unsqueeze(2).to_broadcast([128, 4, 64])

        def v4(ap2d):
            return ap2d.rearrange("p (a b) -> p a b", a=4)

        abank = [0]

        A_BANKS = (2, 3, 6, 7)

        def nextA():
            abank[0] += 1
            return A_BANKS[abank[0] % 4]

        def stageA(g):
            s = g % 2
            hs4 = slice(4 * g, 4 * g + 4)
            blks = [2 * g, 2 * g + 1, 16 + g, 24 + g]
            if g not in zslab:
                zslab[g] = load_slab("z%d" % g)
            slab, bs = zslab[g]
            p.dma("sp", [(normw[g % 3][:, :], cp("normw", g * 256, (g + 1) * 256))], "normw%d" % (g % 3),
                  writes=[b_normw[g % 3]])
            p.dma("sp", [(cbr_f[:, 0:256], cpack[0:64, cbr0 + g * 256:cbr0 + (g + 1) * 256]),
                         (cbr_f[:, 256:384], cpack[0:64, cbr0 + 2048 + g * 128:cbr0 + 2048 + (g + 1) * 128]),
                         (cbr_f[:, 384:512], cpack[0:64, cbr0 + 3072 + g * 128:cbr0 + 3072 + (g + 1) * 128])],
                  "cbr", writes=[b_cbr])
            p.dma("pool", [(selp[s][:, :, :].rearrange("p a b -> p (a b)"), cpack[:, sp0 + g * 512:sp0 + (g + 1) * 512]),
                           (seln[s][:, :, :].rearrange("p a b -> p (a b)"), cpack[:, sn0 + g * 512:sn0 + (g + 1) * 512])],
                  "sel%d" % s, writes=[b_sel[s]])
            CPY("dve", cbst[s][0:64, :], cbr_f[:, :], [b_cbr], [b_cbst[s]])
            TT("dve", cbr_t[:, :], cbr_f[:, :], cbst[s][0:64, :], ALU.subtract, [b_cbr, b_cbst[s]], [b_cbr])
            CPY("dve", cbst[s][32:64, :], cbr_t[32:64, :], [b_cbr], [b_cbst[s]])
            for j in range(4):
                for k in range(4):
                    TS("dve", dg[:, j * 4 + k, :], ident_f[:, :], convw[:, blks[j] * 4 + k:blks[j] * 4 + k + 1], None,
                       ALU.mult, None, [b_const], [b_dg])
            CPY("dve", dhi[:, :], dskip[:, hs4], [b_const], [b_dhl])
            TT("dve", dlo[:, :], dskip[:, hs4], dhi[:, :], ALU.subtract, [b_const, b_dhl], [b_dhl])
            for hl in range(4):
                TS("dve", dI[s][:, hl * 2, :], ident_f[:, :], dhi[:, hl:hl + 1], None, ALU.mult, None,
                   [b_const, b_dhl], [b_dI[s]])
                TS("dve", dI[s][:, hl * 2 + 1, :], ident_f[:, :], dlo[:, hl:hl + 1], None, ALU.mult, None,
                   [b_const, b_dhl], [b_dI[s]])
            yield
            nslab = load_slab("xs%d" % g)
            for c2 in range(8):
                bk = nextA()
                for cc in range(2):
                    c = c2 * 2 + cc
                    for kc in range(16):
                        MM(banks[bk][:, cc * 256:(cc + 1) * 256], xT_bf[:, kc, c * 128:(c + 1) * 128],
                           slab[:, kc, 0:256], kc == 0, kc == 15, [bs, b_xT[kc]], [bkb[bk]],
                           signal=(kc == 15 and cc == 1))
                SILU2(gz_tok[s][:, 2 * c2:2 * c2 + 2, :].rearrange("p a b -> p (a b)"), banks[bk][:, :],
                      [bkb[bk]], [b_gzt[s]])
                yield
            for j in range(4):
                sl = j % 2
                if j == 0:
                    slab, bs = nslab
                    nslab = load_slab("B%d" % g)
                    j0 = 0
                elif j == 1:
                    j0 = 128
                elif j == 2:
                    slab, bs = nslab
                    nslab = load_slab("C%d" % g)
                    j0 = 0
                else:
                    slab, bs = nslab
                    j0 = 0
                    if g < 7:
                        zslab[g + 1] = load_slab("z%d" % (g + 1))
                for g4 in range(4):
                    bk = nextA()
                    for kc in range(16):
                        MM(banks[bk][:, :], slab[:, kc, j0:j0 + 128], xT_bf[:, kc, g4 * 512:(g4 + 1) * 512],
                           kc == 0, kc == 15, [bs, b_xT[kc]], [bkb[bk]])
                    CPY("dve" if g4 % 2 else "act", uT[:, sl, 3 + g4 * 512:3 + (g4 + 1) * 512], banks[bk][:, :],
                        [bkb[bk]], [b_uT[sl]])
                    yield
                if j < 3:
                    for c4 in range(4):
                        bk = nextA()
                        for cc in range(4):
                            c = c4 * 4 + cc
                            for k in range(4):
                                MM(banks[bk][:, cc * 128:(cc + 1) * 128], uT[:, sl, c * 128 + k:c * 128 + k + 128],
                                   dg[:, j * 4 + k, :], cc == 0 and k == 0, False, [b_dg, b_uT[sl]], [bkb[bk]],
                                   signal=False)
                            MM(banks[bk][:, cc * 128:(cc + 1) * 128], ones_bf[:, 0:128],
                               cbst[s][:, j * 128:(j + 1) * 128], False, cc == 3, [b_const, b_cbst[s]], [bkb[bk]],
                               signal=(cc == 3))
                        if j < 2:
                            SILU2(xs_tok[s][:, 4 * c4:4 * c4 + 4, j * 128:(j + 1) * 128], banks[bk][:, :],
                                  [bkb[bk]], [b_xs[s]],
                                  in_view=lambda a: a.rearrange("p (a b) -> p a b", a=4),
                                  tt_view=lambda a: a.rearrange("p (a b) -> p a b", a=4))
                        else:
                            SILU2(B_tok[s][:, 4 * c4:4 * c4 + 4, :].rearrange("p a b -> p (a b)"), banks[bk][:, :],
                                  [bkb[bk]], [b_Btok[s]])
                        yield
                if j >= 2:
                    dst, bd = (BT[s], b_BT[s]) if j == 2 else (CT[s], b_CT[s])
                    for g4 in range(4):
                        bk = nextA()
                        for k in range(4):
                            MM(banks[bk][:, :], dg[:, j * 4 + k, :], uT[:, sl, g4 * 512 + k:g4 * 512 + k + 512],
                               k == 0, False, [b_dg, b_uT[sl]], [bkb[bk]], signal=False)
                        MM(banks[bk][:, :], cbst[s][:, j * 128:(j + 1) * 128], ones_bf[:, :], False, True,
                           [b_cbst[s], b_const], [bkb[bk]])
                        SILU2(dst[:, g4 * 512:(g4 + 1) * 512], banks[bk][:, :], [bkb[bk]], [bd])
                        yield

        def stageB_front(g, c):
            s = g % 2
            i2 = c % 2
            ck = slice(c * 128, (c + 1) * 128)
            hs4 = slice(4 * g, 4 * g + 4)
            cbk = 2 if g == 7 else 5
            MM(banks[cbk][:, 256:384], BT[s][:, ck], CT[s][:, ck], True, True, [b_BT[s], b_CT[s]], [bkb[cbk]])
            ACT(cb_bf[i2][:, :], banks[cbk][:, 256:384], AF.Copy, [bkb[cbk]], [b_cb[i2]], scale=0.25)
            sb = 0
            for hl in range(4):
                MM(banks[sb][:, hl * 128:(hl + 1) * 128], selp[s][:, hl, :], Ast[:, ck],
                   hl == 0, False, [b_sel[s], b_Ast], [bkb[sb]], signal=False)
                MM(banks[sb][:, hl * 128:(hl + 1) * 128], Ast[:, ck], seln[s][:, hl, :],
                   False, False, [b_sel[s], b_Ast], [bkb[sb]], signal=False)
            MM(banks[sb][:, :], ident_bf[:, :], mask4_bf[:, :], False, True, [b_const], [bkb[sb]])
            ACT(dec_bf[i2][:, :], banks[sb][:, :], AF.Exp, [bkb[sb]], [b_dec[i2]])
            TT("dve", MT_bf[i2][:, :].rearrange("p (a b) -> p a b", a=4),
               dec_bf[i2][:, :].rearrange("p (a b) -> p a b", a=4),
               cb_bf[i2][:, :].unsqueeze(1).to_broadcast([128, 4, 128]), ALU.mult,
               [b_dec[i2], b_cb[i2]], [b_MT[i2]])
            TT("pool", v4(Xdt[i2][:, :]), v4(xs_tok[s][:, c, :]), bc4(dt_tok[:, c, hs4]), ALU.mult,
               [b_xs[s], b_ssdc], [b_Xdt[i2]])
            TT("pool", v4(Xdd[i2][:, :]), v4(xs_tok[s][:, c, :]), bc4(dtde[:, c, hs4]), ALU.mult,
               [b_xs[s], b_ssdc], [b_Xdd[i2]])

        def stageB_back(g, c):
            s = g % 2
            i2 = c % 2
            yb = 1
            ck = slice(c * 128, (c + 1) * 128)
            hs4 = slice(4 * g, 4 * g + 4)
            for hl in range(4):
                MM(banks[yb][:, hl * 64:(hl + 1) * 64], MT_bf[i2][:, hl * 128:(hl + 1) * 128],
                   Xdt[i2][:, hl * 64:(hl + 1) * 64], hl == 0, False,
                   [b_MT[i2], b_Xdt[i2]], [bkb[yb]], signal=False)
            for hl in range(4):
                for e2 in range(2):
                    lastm = (hl == 3 and e2 == 1 and c == 0)
                    MM(banks[yb][:, hl * 64:(hl + 1) * 64], dI[s][:, hl * 2 + e2, :],
                       xs_tok[s][:, c, hl * 64:(hl + 1) * 64], False, lastm,
                       [b_dI[s], b_xs[s]], [bkb[yb]], signal=lastm)
            if c > 0:
                MM(banks[yb][:, 256:512], CT[s][:, ck], prev_bf[:, :], False, True, [b_CT[s], b_prev], [bkb[yb]])
            MM(banks[5][:, 0:256], B_tok[s][:, c, :], Xdd[i2][:, :], True, True, [b_Btok[s], b_Xdd[i2]], [bkb[5]])
            if c > 0:
                TT("dve", v4(t1[i2][:, :]), v4(banks[yb][:, 256:512]), bc4(ea_tok[:, c, hs4]), ALU.mult,
                   [bkb[yb], b_ssdc], [b_t1[i2]])
                TT("dve", t2[i2][:, :], t1[i2][:, :], banks[yb][:, 0:256], ALU.add,
                   [b_t1[i2], bkb[yb]], [b_t2[i2]])
            else:
                CPY("dve", t2[i2][:, :], banks[yb][:, 0:256], [bkb[yb]], [b_t2[i2]])
            TT("pool", u_all[:, c, :], t2[i2][:, :], gz_tok[s][:, c, :], ALU.mult, [b_t2[i2], b_gzt[s]], [b_uall[c]])
            ACT(junk[:, :], u_all[:, c, :], AF.Square, [b_uall[c]], [b_junk, b_ss], accum=ss[:, c:c + 1])
            if c < 15:
                if c == 0:
                    TS("dve", carry[:, :], banks[5][:, 0:256], 0.5, None, ALU.mult, None, [bkb[5]], [b_carry])
                else:
                    TT("dve", v4(carry[:, :]), v4(carry[:, :]), bc4(cd_bc[:, c, hs4]), ALU.mult,
                       [b_ssdc], [b_carry])
                    STT("dve", carry[:, :], banks[5][:, 0:256], 0.5, carry[:, :], ALU.mult, ALU.add,
                        [bkb[5]], [b_carry])
                CPY("act", prev_bf[:, :], carry[:, :], [b_carry], [b_prev])

        def stageC_prep(g, c0=0, c1=16):
            TS("dve", rstd[:, c0:c1], ss[:, c0:c1], 1.0 / 256.0, 4.0 * EPS, ALU.mult, ALU.add, [b_ss], [b_rstd])
            ACT(rstd[:, c0:c1], rstd[:, c0:c1], AF.Sqrt, [b_rstd], [b_rstd])
            p.op("dve", lambda e: e.reciprocal(out=rstd[:, c0:c1], in_=rstd[:, c0:c1]), [b_rstd], [b_rstd])

        def stageC_yn(g, c):
            i2 = c % 2
            STT("dve", yn_bf[i2][:, :], u_all[:, c, :], rstd[:, c:c + 1], normw[g % 3][:, :], ALU.mult, ALU.mult,
                [b_uall[c], b_rstd, b_normw[g % 3]], [b_yn[i2]])

        def stageC_tr(g, c):
            i2 = c % 2
            xbv = banks[4].bitcast(BF16)
            o0 = i2 * 256
            for j in range(2):
                TR(xbv[:, o0 + j * 128:o0 + (j + 1) * 128], yn_bf[i2][:, j * 128:(j + 1) * 128], ident_bf[:, :],
                   [b_yn[i2], b_const], [bkb[4]], signal=(j == 1))
            CPY("act" if c % 2 else "dve", ysm[i2][:, :, :],
                xbv[:, o0:o0 + 256].rearrange("p (a b) -> p a b", a=2), [bkb[4]], [b_ysm[i2]])
            c4_, cc_ = c // 4, c % 4
            mix_toks.append(p.dma("sp", [(mixs[c4_, :, 8 + 2 * g + j, cc_ * 128:(cc_ + 1) * 128], ysm[i2][:, j, :])
                                         for j in range(2)],
                                  "ysm%d" % i2, reads=[b_ysm[i2]]))

        for _ in stageA(0):
            pass
        for g in range(8):
            gen = stageA(g + 1) if g < 7 else None
            if g > 0:
                stageC_prep(g - 1)
                stageC_yn(g - 1, 0)
            stageB_front(g, 0)
            for c in range(16):
                if g > 0:
                    stageC_tr(g - 1, c)
                if g > 0 and c + 1 < 16:
                    stageC_yn(g - 1, c + 1)
                if c + 1 < 16:
                    stageB_front(g, c + 1)
                stageB_back(g, c)
                if g == 7 and phases >= 3:
                    p.dma("sp", [(w_bf[:, c, :], wob[:, c, :])], "w%d" % c, reads=[b_wob[c]], writes=[b_w[c]] + b_xT)
                if gen is not None:
                    for _ in range(3):
                        next(gen, None)
            if gen is not None:
                for _ in gen:
                    pass
    if phases >= 2:
        stageC_prep(7)
        for cc in range(16):
            stageC_yn(7, cc)
            stageC_tr(7, cc)
    p.barrier()
    mem.release(mem.base)

    out_toks = []
    if phases >= 3:
        mem.cur = (mem.base + 31) // 32 * 32 + 24 * D * 2
        lng = mem.alloc([128, D], F32, "lng"); lnb = mem.alloc([128, D], F32, "lnb")
        b_ln = Buf("ln")
        mb = [mem.alloc([128, 24, 512], BF16, "mb%d" % i) for i in range(2)]
        b_mb = [Buf("mb%d" % i) for i in range(2)]
        xt = [mem.alloc([128, D], F32, "xt%d" % i) for i in range(2)]
        rr = [mem.alloc([128, D], F32, "rr%d" % i) for i in range(2)]
        b_xt = [Buf("xt%d" % i) for i in range(2)]; b_rr = [Buf("rr%d" % i) for i in range(2)]
        st6 = mem.alloc([128, 4, 6], F32, "st6"); mv = mem.alloc([128, 2], F32, "mv")
        sc2 = mem.alloc([128, 2], F32, "sc2")
        b_st = Buf("st"); b_mv = Buf("mv"); b_sc2 = Buf("sc2")
        p.wait_all("sp", mix_toks)

        def load_mb(tb4):
            ms = tb4 % 2
            p.dma("sp", [(mb[ms][:, :, :].rearrange("p k t -> p (k t)"), mixs[tb4].rearrange("p k t -> p (k t)"))],
                  "mb%d" % ms, writes=[b_mb[ms]])

        def load_x(t):
            p.dma("sp", [(xt[t % 2][:, :], x_tok[t * 128:(t + 1) * 128, :])], "xt%d" % (t % 2), writes=[b_xt[t % 2]])

        load_mb(0)
        p.dma("sp", [(w_bf[:, 16:24, :], wob[:, 16:24, :])], "w16", reads=b_wob[16:24], writes=b_w[16:24])
        load_x(0)
        p.dma("sp", [(lng[:, :], cp("lng")), (lnb[:, :], cp("lnb"))], "ln", writes=[b_ln])
        load_mb(1)
        for tb4 in range(4):
            ms = tb4 % 2
            if tb4 >= 1 and tb4 + 1 < 4:
                load_mb(tb4 + 1)
            for tt in range(4):
                t = tb4 * 4 + tt
                xs_ = t % 2
                if t + 1 < 16:
                    load_x(t + 1)
                bset = (t % 2) * 4
                for kb in range(24):
                    for n in range(4):
                        MM(banks[bset + n][:, :], mb[ms][:, kb, tt * 128:(tt + 1) * 128],
                           w_bf[:, kb, n * 512:(n + 1) * 512], kb == 0, kb == 23, [b_mb[ms], b_w[kb]], [bkb[bset + n]])
                for n in range(4):
                    bk = bset + n
                    STT("dve", rr[xs_][:, n * 512:(n + 1) * 512], xt[xs_][:, n * 512:(n + 1) * 512], ALPHA,
                        banks[bk][:, :], ALU.mult, ALU.add, [b_xt[xs_], bkb[bk]], [b_rr[xs_]])
                    p.op("dve", lambda e, n=n, xs_=xs_: e.bn_stats(out=st6[:, n, :], in_=rr[xs_][:, n * 512:(n + 1) * 512]),
                         [b_rr[xs_]], [b_st])
                p.op("dve", lambda e: e.bn_aggr(out=mv[:, :], in_=st6[:, :, :].rearrange("p a b -> p (a b)")),
                     [b_st], [b_mv])
                TS("dve", sc2[:, 0:1], mv[:, 1:2], EPS, None, ALU.add, None, [b_mv], [b_sc2])
                ACT(sc2[:, 0:1], sc2[:, 0:1], AF.Sqrt, [b_sc2], [b_sc2])
                p.op("dve", lambda e: e.reciprocal(out=sc2[:, 0:1], in_=sc2[:, 0:1]), [b_sc2], [b_sc2])
                TS("dve", rr[xs_][:, :], rr[xs_][:, :], mv[:, 0:1], sc2[:, 0:1], ALU.subtract, ALU.mult,
                   [b_mv, b_sc2], [b_rr[xs_]])
                TT("pool", rr[xs_][:, :], rr[xs_][:, :], lng[:, :], ALU.mult, [b_ln], [b_rr[xs_]])
                TT("pool", rr[xs_][:, :], rr[xs_][:, :], lnb[:, :], ALU.add, [b_ln], [b_rr[xs_]])
                out_toks.append(p.dma("sp", [(out[t * 128:(t + 1) * 128, :], rr[xs_][:, :])], "rr%d" % xs_,
                                      reads=[b_rr[xs_]]))
    p.wait_all("sp", out_toks + mix_toks)
    p.emit()
    return nc, p, mem


_CACHE = {}


def kernel(x, w_in, b_forget, conv_w, conv_b, dt_bias, a_log, d_skip, ssd_norm_w, w_out, ln_g, ln_b):
    x = np.asarray(x, np.float32)
    wpack = make_wpack(np.asarray(w_in, np.float32)[0])
    cpack = make_cpack(np.asarray(b_forget, np.float32)[0], np.asarray(conv_w, np.float32)[0],
                       np.asarray(conv_b, np.float32)[0], np.asarray(dt_bias, np.float32)[0],
                       np.asarray(a_log, np.float32)[0], np.asarray(d_skip, np.float32)[0],
                       np.asarray(ssd_norm_w, np.float32)[0], np.asarray(ln_g, np.float32)[0],
                       np.asarray(ln_b, np.float32)[0])
    wo = np.ascontiguousarray(np.asarray(w_out, np.float32)[0])
    nc, _, _ = build_program()
    in_maps = []
    for b in range(NCORES):
        in_maps.append({"x_tok": np.ascontiguousarray(x[b]), "xT": np.ascontiguousarray(x[b].T),
                        "wpack": wpack, "w_out": wo, "cpack": cpack})
    res = run_bass_kernel_spmd(nc, in_maps, core_ids=list(range(NCORES)))
    return np.stack([np.asarray(r["out"], np.float32) for r in res.results], axis=0)
```

```python
import math
import numpy as np
import ml_dtypes
import concourse.bass as bass
import concourse.mybir as mybir
from concourse.bass_utils import run_bass_kernel_spmd

F32 = mybir.dt.float32
BF16 = mybir.dt.bfloat16
AF = mybir.ActivationFunctionType
ALU = mybir.AluOpType

L = 2048
D = 2048
NT = 16
NCORES = 8
SCALE = 1.0 / math.sqrt(128.0)
NEG = -30000.0
ALPHA = 2.0 ** 0.25
EPS = 1e-5

ENGS = ("pe", "act", "dve", "pool", "sp")


class Buf:
    __slots__ = ("name", "w", "r", "excl")

    def __init__(self, name, excl=False):
        self.name = name
        self.w = None
        self.r = {}
        self.excl = excl


class Prog:
    def __init__(self, nc):
        self.nc = nc
        self.ops = {e: [] for e in ENGS}
        self.cnt = {}
        self.known = {e: {} for e in ENGS}
        self.snap = {}
        self.sem_order = []
        self.nwaits = 0

    def _sem(self, key):
        if key not in self.cnt:
            self.cnt[key] = 0
            self.sem_order.append(key)

    def _collect(self, eng, reads, writes):
        waits = {}
        known = self.known[eng]

        def need(dep):
            if dep is None:
                return
            k, v = dep
            if eng == "pe" and k == "pe":
                return
            if known.get(k, 0) >= v:
                return
            if waits.get(k, 0) < v:
                waits[k] = v

        for b in reads:
            need(b.w)
        for b in writes:
            need(b.w)
            for d in b.r.items():
                need(d)
        for k, v in list(waits.items()):
            if known.get(k, 0) < v:
                known[k] = v
            s = self.snap.get((k, v))
            if s:
                for k2, v2 in s.items():
                    if known.get(k2, 0) < v2:
                        known[k2] = v2
        return {k: v for k, v in waits.items() if known.get(k, 0) <= v}

    def _mark(self, tok, reads, writes):
        for b in reads:
            if b.r.get(tok[0], 0) < tok[1]:
                b.r[tok[0]] = tok[1]
        for b in writes:
            b.w = tok
            b.r = {}

    def op(self, eng, fn, reads=(), writes=(), signal=True):
        self._sem(eng)
        ex = [b for b in reads if b.excl and b not in writes]
        if ex:
            writes = list(writes) + ex
        waits = self._collect(eng, reads, writes)
        if signal:
            self.cnt[eng] += 1
            tok = (eng, self.cnt[eng])
            self.snap[tok] = dict(self.known[eng])
        else:
            tok = (eng, self.cnt[eng] + 1)
        self._mark(tok, reads, writes)
        self.nwaits += len(waits)
        self.ops[eng].append((waits, fn, (eng, 1) if signal else None))
        return tok

    def dma(self, eng, pairs, key, reads=(), writes=(), **kw):
        key = "d:" + key
        self._sem(key)
        waits = self._collect(eng, reads, writes)
        for i, (o, a) in enumerate(pairs):
            def fn(e, o=o, a=a):
                return e.dma_start(out=o, in_=a, **kw)
            self.ops[eng].append((waits if i == 0 else {}, fn, (key, 16)))
        self.cnt[key] += 16 * len(pairs)
        tok = (key, self.cnt[key])
        self.snap[tok] = dict(self.known[eng])
        self._mark(tok, reads, writes)
        self.nwaits += len(waits)
        return tok

    def wait_all(self, eng, toks):
        waits = {}
        for k, v in toks:
            if self.known[eng].get(k, 0) < v and waits.get(k, 0) < v:
                waits[k] = v
        for k, v in waits.items():
            self.known[eng][k] = v
        self.ops[eng].append((waits, None, None))

    def barrier(self):
        toks = [(k, v) for k, v in self.cnt.items() if v > 0]
        for e in ENGS:
            self.wait_all(e, toks)

    def emit(self):
        nc = self.nc
        from contextlib import ExitStack
        with ExitStack() as st:
            sems = {}
            for k in self.sem_order:
                sems[k] = st.enter_context(nc.semaphore("s%d" % len(sems)))
            block = st.enter_context(nc.Block())

            def run(ename):
                def body(e):
                    for waits, fn, inc in self.ops[ename]:
                        for k, v in waits.items():
                            e.wait_ge(sems[k], v)
                        if fn is None:
                            continue
                        ins = fn(e)
                        if inc is not None:
                            ins.then_inc(sems[inc[0]], inc[1])
                return body

            block.tensor(run("pe"))
            block.scalar(run("act"))
            block.vector(run("dve"))
            block.gpsimd(run("pool"))
            block.sync(run("sp"))


class Mem:
    def __init__(self, nc, base, top):
        self.nc = nc
        self.base = base
        self.top = top
        self.cur = base
        self.n = 0
        self.peak = base

    def alloc(self, shape, dtype, name="t"):
        esz = 2 if dtype == BF16 else 4
        nbytes = int(np.prod(shape[1:])) * esz
        off = (self.cur + 31) // 32 * 32
        assert off + nbytes <= self.top, "SBUF overflow %s need %d at %d top %d" % (name, nbytes, off, self.top)
        self.cur = off + nbytes
        self.peak = max(self.peak, self.cur)
        self.n += 1
        return self.nc.alloc_sbuf_tensor_at("%s_%d" % (name, self.n), list(shape), dtype, offset=off)

    def mark(self):
        return self.cur

    def release(self, mark):
        self.cur = mark


Q0, K0, V0, F0, ZA0, ZS0, XBC0, DT0 = 0, 1024, 2048, 3072, 3080, 4104, 6152, 10248


def slab_plan():
    plan = []
    f = [-1] * 128
    dtc = [-1] * 128
    for r in (0, 32, 64):
        for i in range(8):
            f[r + i] = F0 + i
        for i in range(32):
            dtc[r + i] = DT0 + i
    plan.append(("f", f))
    plan.append(("dt", dtc))
    for hp in range(4):
        plan.append(("v%d" % hp, list(range(V0 + hp * 256, V0 + (hp + 1) * 256))))
        for h in (2 * hp, 2 * hp + 1):
            plan.append(("q%d" % h, list(range(Q0 + h * 128, Q0 + (h + 1) * 128))))
            plan.append(("k%d" % h, list(range(K0 + h * 128, K0 + (h + 1) * 128))))
            plan.append(("za%d" % h, list(range(ZA0 + h * 128, ZA0 + (h + 1) * 128))))
    for g in range(8):
        plan.append(("z%d" % g, list(range(ZS0 + g * 256, ZS0 + (g + 1) * 256))))
        plan.append(("xs%d" % g, list(range(XBC0 + g * 256, XBC0 + (g + 1) * 256))))
        plan.append(("B%d" % g, list(range(XBC0 + 2048 + g * 128, XBC0 + 2048 + (g + 1) * 128))))
        plan.append(("C%d" % g, list(range(XBC0 + 3072 + g * 128, XBC0 + 3072 + (g + 1) * 128))))
    offs = {}
    o = 0
    for name, cols in plan:
        offs[name] = (o, len(cols))
        o += 16 * len(cols)
    return plan, offs, o


CP_FIELDS = [("ident", 128), ("mask4", 512), ("ones", 512), ("selAneg", 1024), ("selS", 4096), ("selSneg", 4096),
             ("prm3", 3), ("dskip", 32), ("normw", 2048), ("lng", 2048),
             ("lnb", 2048), ("convw", 128), ("convbc", 32), ("convbr", 4096)]


def cp_layout():
    offs = {}
    o = 0
    for n, w in CP_FIELDS:
        offs[n] = (o, w)
        o += w
    return offs, o


def make_wpack(w_in):
    plan, offs, tot = slab_plan()
    wp = np.zeros((128, tot), np.float32)
    for name, cols in plan:
        o, W = offs[name]
        blk = np.zeros((D, W), np.float32)
        idx = np.array(cols)
        ok = idx >= 0
        blk[:, ok] = w_in[:, idx[ok]]
        wp[:, o:o + 16 * W] = blk.reshape(16, 128, W).transpose(1, 0, 2).reshape(128, 16 * W)
    return wp


def make_cpack(b_forget, conv_w, conv_b, dt_bias, a_log, d_skip, ssd_norm_w, ln_g, ln_b):
    offs, tot = cp_layout()
    cp = np.zeros((128, tot), np.float32)

    def put(name, arr):
        o, w = offs[name]
        cp[:, o:o + w] = arr

    put("ident", np.eye(128, dtype=np.float32))
    s = np.arange(128)[:, None]
    l = np.arange(128)[None, :]
    maskT = np.where(l >= s, 0.0, NEG).astype(np.float32)
    put("mask4", np.tile(maskT, (1, 4)))
    put("ones", np.ones((128, 512), np.float32))
    selA = np.zeros((128, 8, 128), np.float32)
    selS = np.zeros((128, 32, 128), np.float32)
    for k in range(96):
        selS[k, k % 32, :] = 1.0
        if k < 72 and (k % 32) < 8:
            selA[k, k % 32, :] = -1.0
    put("selAneg", selA.reshape(128, -1))
    put("selS", selS.reshape(128, -1))
    put("selSneg", -selS.reshape(128, -1))
    bf = np.zeros((128, 1), np.float32)
    dtb = np.zeros((128, 1), np.float32)
    alog = np.zeros((128, 1), np.float32)
    for r in (0, 32, 64):
        bf[r:r + 8, 0] = b_forget
        dtb[r:r + 32, 0] = dt_bias
        alog[r:r + 32, 0] = a_log
    put("prm3", np.concatenate([bf, dtb, alog], axis=1))
    put("dskip", np.tile(d_skip[None, :], (128, 1)))
    put("normw", np.tile(ssd_norm_w[None, :], (128, 1)))
    put("lng", np.tile(ln_g[None, :], (128, 1)))
    put("lnb", np.tile(ln_b[None, :], (128, 1)))
    put("convw", conv_w.T.reshape(32, 128, 4).transpose(1, 0, 2).reshape(128, 128))
    put("convbc", conv_b.reshape(32, 128).T)
    cbr = np.zeros((128, 4096), np.float32)
    cbr[0] = conv_b
    cbr[32] = conv_b
    put("convbr", cbr)
    return cp


def build_program(dbg=False, phases=3):
    nc = bass.Bass("TRN2", target_bir_lowering=False)
    plan, woffs, wtot = slab_plan()
    coffs, ctot = cp_layout()

    x_tok = nc.dram_tensor("x_tok", [L, D], F32, kind="ExternalInput").ap()
    xT = nc.dram_tensor("xT", [D, L], F32, kind="ExternalInput").ap()
    wpack = nc.dram_tensor("wpack", [128, wtot], F32, kind="ExternalInput").ap()
    w_out = nc.dram_tensor("w_out", [3072, D], F32, kind="ExternalInput").ap()
    cpack = nc.dram_tensor("cpack", [128, ctot], F32, kind="ExternalInput").ap()
    out = nc.dram_tensor("out", [L, D], F32, kind="ExternalOutput").ap()
    mixs = nc.dram_tensor("mixs", [4, 128, 24, 512], BF16, kind="ExternalOutput" if dbg else "Internal").ap()

    wob = nc.dram_tensor("wob", [128, 24, D], BF16, kind="Internal").ap()
    b_wob = [Buf("wob%d" % i) for i in range(24)]
    p = Prog(nc)
    mem = Mem(nc, nc.sbuf_base, nc.sbuf_top)
    banks = [nc.alloc_psum_tensor("bank%d" % i, [128, 512], F32) for i in range(8)]
    bkb = [Buf("bank%d" % i, excl=True) for i in range(8)]

    def cp(name, a=0, b=None):
        o, w = coffs[name]
        if b is None:
            b = w
        return cpack[:, o + a:o + b]

    def MM(outap, lhsT, rhs, start, stop, reads, writes, signal=None):
        if signal is None:
            signal = stop
        return p.op("pe", lambda e: e.matmul(outap, lhsT, rhs, start=start, stop=stop), reads, writes, signal)

    def TR(outap, in_, ident, reads, writes, signal=True):
        return p.op("pe", lambda e: e.transpose(outap, in_, ident), reads, writes, signal)

    def ACT(outap, in_, func, reads, writes, bias=None, scale=1.0, accum=None):
        def fn(e):
            kw = {}
            if bias is not None:
                kw["bias"] = bias
            if accum is not None:
                kw["accum_out"] = accum
            return e.activation(out=outap, in_=in_, func=func, scale=scale, **kw)
        return p.op("act", fn, reads, writes)

    def TT(eng, outap, in0, in1, op, reads, writes):
        return p.op(eng, lambda e: e.tensor_tensor(out=outap, in0=in0, in1=in1, op=op), reads, writes)

    def TS(eng, outap, in0, s1, s2, op0, op1, reads, writes):
        def fn(e):
            if s2 is None:
                return e.tensor_scalar(out=outap, in0=in0, scalar1=s1, scalar2=None, op0=op0)
            return e.tensor_scalar(out=outap, in0=in0, scalar1=s1, scalar2=s2, op0=op0, op1=op1)
        return p.op(eng, fn, reads, writes)

    def STT(eng, outap, in0, scalar, in1, op0, op1, reads, writes):
        return p.op(eng, lambda e: e.scalar_tensor_tensor(out=outap, in0=in0, scalar=scalar, in1=in1,
                                                          op0=op0, op1=op1), reads, writes)

    def CPY(eng, outap, in_, reads, writes):
        if eng == "act":
            return p.op("act", lambda e: e.copy(out=outap, in_=in_), reads, writes)
        return p.op(eng, lambda e: e.tensor_copy(out=outap, in_=in_), reads, writes)

    b_xT = [Buf("xT%d" % i) for i in range(16)]
    xT_bf = mem.alloc([128, 16, L], BF16, "xT_bf")
    w_bf = nc.alloc_sbuf_tensor_at("w_out_bf", [128, 24, D], BF16, offset=(mem.base + 31) // 32 * 32)
    b_w = [Buf("w%d" % i) for i in range(24)]
    w_v = w_out.rearrange("(kb p) n -> p kb n", p=128)
    ident_bf = mem.alloc([128, 128], BF16, "ident_bf")
    ident_f = mem.alloc([128, 128], F32, "ident_f")
    mask4_bf = mem.alloc([128, 512], BF16, "mask4")
    ones_bf = mem.alloc([128, 512], BF16, "ones_bf")
    ones_f = mem.alloc([128, 128], F32, "ones_f")
    prm = mem.alloc([128, 8], F32, "prm")
    dskip = mem.alloc([128, 32], F32, "dskip")
    convw = mem.alloc([128, 128], F32, "convw")
    convbc = mem.alloc([128, 32], F32, "convbc")
    b_const = Buf("const")
    b_prm = Buf("prm")
    slabs = [mem.alloc([128, 16, 256], BF16, "slab%d" % i) for i in range(2)]
    b_slab = [Buf("slab%d" % i) for i in range(2)]
    Ast = mem.alloc([128, L], BF16, "Ast")
    dt_tok = mem.alloc([128, 16, 32], F32, "dt_tok")
    ea_tok = mem.alloc([128, 16, 32], F32, "ea_tok")
    cd_bc = mem.alloc([128, 16, 32], F32, "cd_bc")
    dtde = mem.alloc([128, 16, 32], F32, "dtde")
    b_ssdc = Buf("ssdc")
    ttb = [mem.alloc([128, 512], F32, "tt%d" % i) for i in range(2)]
    b_tt = [Buf("tt%d" % i) for i in range(2)]
    tti = [0]

    def SILU2(outap, psum_ap, reads, writes, in_view=None, tt_view=None):
        k = tti[0] % 2
        tti[0] += 1
        ACT(ttb[k][:, :], psum_ap, AF.Tanh, reads, [b_tt[k]], scale=0.5)
        tv = ttb[k][:, :] if tt_view is None else tt_view(ttb[k][:, :])
        pv = psum_ap if in_view is None else in_view(psum_ap)
        STT("dve", outap, tv, 1.0, pv, ALU.add, ALU.mult, [b_tt[k]] + list(reads), writes)

    p.dma("pool", [(ident_bf[:, :], cp("ident")), (mask4_bf[:, :], cp("mask4")), (ones_bf[:, :], cp("ones")),
                   ], "constbf", writes=[b_const])
    p.dma("sp", [(ident_f[:, :], cp("ident")), (ones_f[:, :], cp("ones", 0, 128)), (prm[:, 0:3], cp("prm3")),
                 (dskip[:, :], cp("dskip")),
                 (convw[:, :], cp("convw")), (convbc[:, :], cp("convbc"))], "constf", writes=[b_const, b_prm])
    slab_i = [0]

    def load_slab(name):
        i = slab_i[0] % 2
        slab_i[0] += 1
        o, W = woffs[name]
        dst = slabs[i][:, :, 0:W] if W == 256 else slabs[i][:, :, 0:W]
        src = wpack[:, o:o + 16 * W].rearrange("p (kc w) -> p kc w", w=W)
        p.dma("pool", [(dst, src)], "slab%d" % i, writes=[b_slab[i]])
        return slabs[i], b_slab[i]

    slab_f, bs_f = load_slab("f")
    slab_dt, bs_dt = load_slab("dt")
    xTv = xT.rearrange("(kc p) l -> p kc l", p=128)
    for kc in range(16):
        p.dma("pool", [(xT_bf[:, kc, :], xTv[:, kc, :])], "xT%d" % kc, writes=[b_xT[kc]])

    def proj_fm(slab, bslab, j0, bank_ids, evac):
        for g4 in range(4):
            bk = bank_ids[g4 % len(bank_ids)]
            for kc in range(16):
                MM(banks[bk][:, :], slab[:, kc, j0:j0 + 128], xT_bf[:, kc, g4 * 512:(g4 + 1) * 512],
                   kc == 0, kc == 15, [bslab, b_xT[kc]], [bkb[bk]])
            evac(g4, bk)

    m0 = mem.mark()
    Fst = mem.alloc([128, L], BF16, "Fst")
    selA_bf = mem.alloc([128, 8, 128], BF16, "selA")
    G_tok = mem.alloc([128, 16, 8], F32, "G_tok")
    qT = [mem.alloc([128, L], BF16, "qT0")]
    kT = [mem.alloc([128, L], BF16, "kT0")]
    gz = [mem.alloc([128, L], F32, "gz0")]
    vp = [mem.alloc([128, 16, 256], BF16, "vp0")]
    b_q = [Buf("q%d" % i) for i in range(2)]; b_k = [Buf("k%d" % i) for i in range(2)]
    b_gz = [Buf("gz%d" % i) for i in range(2)]; b_vp = [Buf("vp%d" % i) for i in range(2)]
    p1_slabs = [mem.alloc([128, 16, 256], BF16, "xslab%d" % i) for i in range(2)] + slabs
    p1_bufs = [Buf("slab%d" % i) for i in (2, 3)] + b_slab
    p1_keys = ["slab2", "slab3", "slab0", "slab1"]
    p1_order = []
    for hp_ in range(4):
        p1_order.append("v%d" % hp_)
        for h_ in (2 * hp_, 2 * hp_ + 1):
            p1_order += ["q%d" % h_, "k%d" % h_, "za%d" % h_]
    p1_issued = {}
    p1_state = [0, 0]

    def p1_fill():
        while p1_state[0] < len(p1_order) and p1_state[0] - p1_state[1] < 4:
            i = p1_state[0]
            nm = p1_order[i]
            o, W = woffs[nm]
            src = wpack[:, o:o + 16 * W].rearrange("p (kc w) -> p kc w", w=W)
            p.dma("pool", [(p1_slabs[i % 4][:, :, 0:W], src)], p1_keys[i % 4], writes=[p1_bufs[i % 4]])
            p1_issued[nm] = (p1_slabs[i % 4], p1_bufs[i % 4])
            p1_state[0] += 1

    def p1_get(nm):
        assert p1_order[p1_state[1]] == nm
        if nm not in p1_issued:
            p1_fill()
        p1_state[1] += 1
        return p1_issued[nm]

    def proj_v(hp):
        slab, bs = p1_get("v%d" % hp)
        vs = hp % 2
        for t2 in range(8):
            bk = t2 % 2
            for tt in range(2):
                t = t2 * 2 + tt
                for kc in range(16):
                    MM(banks[bk][:, tt * 256:(tt + 1) * 256], xT_bf[:, kc, t * 128:(t + 1) * 128],
                       slab[:, kc, 0:256], kc == 0, kc == 15, [bs, b_xT[kc]], [bkb[bk]],
                       signal=(kc == 15 and tt == 1))
            eng = "act" if (t2 % 2 == 0 or hp == 0) else "dve"
            CPY(eng, vp[vs][:, 2 * t2:2 * t2 + 2, :].rearrange("p a b -> p (a b)"), banks[bk][:, :],
                [bkb[bk]], [b_vp[vs]])

    def proj_q(h):
        hs = h % 2
        slab, bs = p1_get("q%d" % h)
        proj_fm(slab, bs, 0, [0, 1], lambda g4, bk: ACT(
            qT[hs][:, g4 * 512:(g4 + 1) * 512], banks[bk][:, :], AF.Copy, [bkb[bk]], [b_q[hs]], scale=SCALE))

    def proj_k(h):
        hs = h % 2
        slab, bs = p1_get("k%d" % h)
        proj_fm(slab, bs, 0, [0, 1], lambda g4, bk: CPY(
            "dve", kT[hs][:, g4 * 512:(g4 + 1) * 512], banks[bk][:, :], [bkb[bk]], [b_k[hs]]))

    def proj_za(h):
        hs = h % 2
        slab, bs = p1_get("za%d" % h)
        proj_fm(slab, bs, 0, [0, 1], lambda g4, bk: SILU2(
            gz[hs][:, g4 * 512:(g4 + 1) * 512], banks[bk][:, :], [bkb[bk]], [b_gz[hs]]))

    m1 = mem.mark()
    Gf = mem.alloc([128, L], F32, "Gf")
    dtA = mem.alloc([128, L], F32, "dtA")
    r1 = mem.alloc([128, L], F32, "r1")
    tmpb = mem.alloc([128, L], BF16, "tmpb")
    b_G = Buf("G"); b_r1 = Buf("r1"); b_tmpb = Buf("tmpb")
    b_F = Buf("Fst"); b_Gtok = Buf("Gtok")
    p.dma("pool", [(selA_bf[:, :, :].rearrange("p a b -> p (a b)"), cp("selAneg"))], "selA", writes=[b_const])

    TS("dve", prm[:, 3:4], prm[:, 0:1], -1.0, None, ALU.mult, None, [b_const], [b_prm])
    ACT(prm[:, 5:6], prm[:, 2:3], AF.Exp, [b_prm], [b_prm])
    TS("dve", prm[:, 4:5], prm[:, 5:6], -1.0, None, ALU.mult, None, [b_prm], [b_prm])

    def split3(src, bsrc, dst, bdst):
        CPY("dve", dst[:, :], src[:, :], [bsrc], [bdst])
        TT("dve", r1[:, :], src[:, :], dst[:, :], ALU.subtract, [bsrc, bdst], [b_r1])
        CPY("dve", dst[32:64, :], r1[32:64, :], [b_r1], [bdst])
        CPY("dve", tmpb[64:96, :], r1[64:96, :], [b_r1], [b_tmpb])
        TT("dve", r1[64:96, :], r1[64:96, :], tmpb[64:96, :], ALU.subtract, [b_r1, b_tmpb], [b_r1])
        CPY("dve", dst[64:96, :], r1[64:96, :], [b_r1], [bdst])

    dtT = mem.alloc([128, L], F32, "dtT")
    acsT = mem.alloc([128, L], F32, "acsT")
    a_tok = mem.alloc([128, 16, 32], F32, "a_tok")
    dtA_tok = mem.alloc([128, 16, 32], F32, "dtA_tok")
    b_dtT = Buf("dtT"); b_acs = Buf("acs"); b_dtA = Buf("dtA"); b_atok = Buf("atok"); b_Ast = Buf("Ast")
    fl = lambda t: t[:, :, :].rearrange("p a b -> p (a b)")
    for kc in range(16):
        for u in range(8):
            sl_, bs_ = (slab_f, bs_f) if u < 4 else (slab_dt, bs_dt)
            g4 = u % 4
            MM(banks[u][:, :], sl_[:, kc, 0:128], xT_bf[:, kc, g4 * 512:(g4 + 1) * 512],
               kc == 0, kc == 15, [bs_, b_xT[kc]], [bkb[u]])
    for g4 in range(4):
        ACT(Gf[:, g4 * 512:(g4 + 1) * 512], banks[g4][:, :], AF.Exp, [bkb[g4], b_prm], [b_G],
            bias=prm[:, 3:4], scale=-1.0)
    for g4 in range(4):
        ACT(dtT[:, g4 * 512:(g4 + 1) * 512], banks[4 + g4][:, :], AF.Exp, [bkb[4 + g4], b_prm], [b_dtT],
            bias=prm[:, 1:2])
    p1_fill()
    ACT(Gf[:, :], Gf[:, :], AF.Ln, [b_G], [b_G], bias=1.0)
    p.op("dve", lambda e: e.tensor_tensor_scan(out=Gf[:, :], data0=ones_f[:, 0:1].to_broadcast([128, L]),
                                               data1=Gf[:, :], initial=0.0, op0=ALU.mult, op1=ALU.add),
         [b_G, b_const], [b_G])
    split3(Gf, b_G, Fst, b_F)
    ACT(dtT[:, :], dtT[:, :], AF.Ln, [b_dtT], [b_dtT], bias=1.0)
    TS("dve", dtA[:, :], dtT[:, :], prm[:, 4:5], None, ALU.mult, None, [b_dtT, b_prm], [b_dtA])
    for c in range(16):
        p.op("dve", lambda e, c=c: e.tensor_tensor_scan(
            out=acsT[:, c * 128:(c + 1) * 128], data0=ones_f[:, 0:1].to_broadcast([128, 128]),
            data1=dtA[:, c * 128:(c + 1) * 128], initial=0.0, op0=ALU.mult, op1=ALU.add), [b_dtA, b_const], [b_acs])
    split3(acsT, b_acs, Ast, b_Ast)
    if phases >= 1:
        proj_v(0)
        proj_q(0)
    for t in range(16):
        TR(banks[2][:, t * 8:(t + 1) * 8], Gf[0:8, t * 128:(t + 1) * 128], ident_f[0:8, 0:8],
           [b_G, b_const], [bkb[2]], signal=(t == 15))
    for (src, bsrc, bk) in ((dtT, b_dtT, 3), (acsT, b_acs, 4), (dtA, b_dtA, 5)):
        for c in range(16):
            TR(banks[bk][:, c * 32:(c + 1) * 32], src[0:32, c * 128:(c + 1) * 128], ident_f[0:32, 0:32],
               [bsrc, b_const], [bkb[bk]], signal=(c == 15))
    CPY("dve", G_tok[:, :, :].rearrange("p a b -> p (a b)"), banks[2][:, 0:128], [bkb[2]], [b_Gtok])
    CPY("dve", fl(dt_tok), banks[3][:, :], [bkb[3]], [b_ssdc])
    CPY("dve", fl(a_tok), banks[4][:, :], [bkb[4]], [b_atok])
    CPY("dve", fl(dtA_tok), banks[5][:, :], [bkb[5]], [b_atok])
    for c in range(16):
        MM(banks[6][:, c * 32:(c + 1) * 32], ones_f[:, :], dtA_tok[:, c, :], c == 0, c == 15,
           [b_atok, b_const], [bkb[6]])
    ACT(fl(cd_bc), banks[6][:, :], AF.Exp, [bkb[6]], [b_ssdc])
    ACT(fl(ea_tok), fl(a_tok), AF.Exp, [b_atok], [b_ssdc])
    TT("dve", fl(a_tok), banks[6][:, :], fl(a_tok), ALU.subtract, [bkb[6], b_atok], [b_atok])
    ACT(fl(dtde), fl(a_tok), AF.Exp, [b_atok], [b_ssdc])
    TT("dve", fl(dtde), fl(dtde), fl(dt_tok), ALU.mult, [b_ssdc], [b_ssdc])
    for tbl in (dt_tok, dtde, ea_tok):
        TS("dve", fl(tbl), fl(tbl), 0.5, None, ALU.mult, None, [b_ssdc], [b_ssdc])
    TS("dve", dskip[:, :], dskip[:, :], 0.5, None, ALU.mult, None, [b_const], [b_const])
    if phases >= 1:
        proj_k(0)
        proj_za(0)
    p.barrier()
    mem.release(m1)

    mix_toks = []
    zslab = {}
    if phases >= 1:
        qT.append(mem.alloc([128, L], BF16, "qT1"))
        kT.append(mem.alloc([128, L], BF16, "kT1"))
        gz.append(mem.alloc([128, L], F32, "gz1"))
        vp.append(mem.alloc([128, 16, 256], BF16, "vp1"))
        Pb = [mem.alloc([128, 512], BF16, "P%d" % i) for i in range(3)]
        rinv = [mem.alloc([128, 512], F32, "rinv%d" % i) for i in range(2)]
        tO = [mem.alloc([128, 512], F32, "tO%d" % i) for i in range(2)]
        attst = [mem.alloc([128, L], BF16, "attst%d" % i) for i in range(2)]
        b_P = [Buf("P%d" % i) for i in range(3)]; b_rinv = [Buf("rinv%d" % i) for i in range(2)]
        b_tO = [Buf("tO%d" % i) for i in range(2)]; b_att = [Buf("att%d" % i) for i in range(2)]
        for h in range(8):
            hs = h % 2
            vs = (h // 2) % 2
            hl = h % 2
            if h > 0:
                if h % 2 == 0:
                    proj_v(h // 2)
                proj_q(h)
                proj_k(h)
                proj_za(h)
            if h == 7 and phases >= 2:
                zslab[0] = load_slab("z0")

            p1_fill()
            if phases >= 3:
                for kb in range(3 * h, 3 * h + 3):
                    p.dma("pool", [(wob[:, kb, :], w_v[:, kb, :])], "wob%d" % kb, writes=[b_wob[kb]])
            steps = [(g, kb) for g in range(4) for kb in range(4 * g + 4)]

            def S_mm(i):
                g, kb = steps[i]
                sb = 2 + (i % 2)
                j = kb - 4 * g
                c0 = j * 128 if j > 0 else 0
                n = 512 - c0
                diag = j >= 0
                q0 = g * 512 + c0
                MM(banks[sb][:, c0:512], kT[hs][:, kb * 128:(kb + 1) * 128], qT[hs][:, q0:q0 + n],
                   True, False, [b_k[hs], b_q[hs]], [bkb[sb]], signal=False)
                MM(banks[sb][:, c0:512], selA_bf[:, h, :], Fst[:, q0:q0 + n],
                   False, not diag, [b_const, b_F], [bkb[sb]], signal=not diag)
                if diag:
                    MM(banks[sb][:, c0:c0 + 128], ident_bf[:, :], mask4_bf[:, 0:128],
                       False, True, [b_const], [bkb[sb]], signal=True)

            def E_act(i):
                g, kb = steps[i]
                sb = 2 + (i % 2)
                j = kb - 4 * g
                c0 = j * 128 if j > 0 else 0
                ps = i % 3
                ACT(Pb[ps][:, c0:512], banks[sb][:, c0:512], AF.Exp, [bkb[sb], b_Gtok], [b_P[ps]],
                    bias=G_tok[:, kb, h:h + 1])

            def PV_mm(i):
                g, kb = steps[i]
                j = kb - 4 * g
                c0 = j * 128 if j > 0 else 0
                ps = i % 3
                ob = 4 + (g % 2)
                rb = 6 + (g % 2)
                last = kb == 4 * g + 3
                MM(banks[ob][:, c0:512], vp[vs][:, kb, hl * 128:(hl + 1) * 128], Pb[ps][:, c0:512],
                   kb == 0, last, [b_vp[vs], b_P[ps]], [bkb[ob]])
                MM(banks[rb][:, c0:512], ones_bf[:, 0:128], Pb[ps][:, c0:512],
                   kb == 0, last, [b_const, b_P[ps]], [bkb[rb]])
                if last:
                    gs = g % 2
                    p.op("dve", lambda e: e.reciprocal(out=rinv[gs][:, :], in_=banks[rb][:, :]),
                         [bkb[rb]], [b_rinv[gs]])
                    STT("dve", tO[gs][:, :], banks[ob][:, :], 0.5, rinv[gs][:, :], ALU.mult, ALU.mult,
                       [bkb[ob], b_rinv[gs]], [b_tO[gs]])
                    TT("pool", attst[hs][:, g * 512:(g + 1) * 512], tO[gs][:, :], gz[hs][:, g * 512:(g + 1) * 512],
                       ALU.mult, [b_tO[gs], b_gz[hs]], [b_att[hs]])

            S_mm(0)
            for i in range(len(steps)):
                E_act(i)
                if i + 1 < len(steps):
                    S_mm(i + 1)
                PV_mm(i)
            mix_toks.append(p.dma("sp", [(mixs[q4, :, h, :], attst[hs][:, q4 * 512:(q4 + 1) * 512]) for q4 in range(4)],
                                  "att%d" % hs, reads=[b_att[hs]]))
    p.barrier()
    mem.release(m0)

    if phases >= 2:
        uT = mem.alloc([128, 2, L + 3], BF16, "uT")
        dg = mem.alloc([128, 16, 128], BF16, "dg")
        dI = [mem.alloc([128, 8, 128], BF16, "dI%d" % i) for i in range(2)]
        selp = [mem.alloc([128, 4, 128], BF16, "selp%d" % i) for i in range(2)]
        seln = [mem.alloc([128, 4, 128], BF16, "seln%d" % i) for i in range(2)]
        BT = [mem.alloc([128, L], BF16, "BT%d" % i) for i in range(2)]
        CT = [mem.alloc([128, L], BF16, "CT%d" % i) for i in range(2)]
        B_tok = [mem.alloc([128, 16, 128], BF16, "B_tok%d" % i) for i in range(2)]
        xs_tok = [mem.alloc([128, 16, 256], BF16, "xs_tok%d" % i) for i in range(2)]
        gz_tok = [mem.alloc([128, 16, 256], BF16, "gz_tok%d" % i) for i in range(2)]
        normw = [mem.alloc([128, 256], F32, "normw%d" % i) for i in range(3)]
        cbst = [mem.alloc([128, 512], BF16, "cbst%d" % i) for i in range(2)]
        u_all = mem.alloc([128, 16, 256], BF16, "u_all")
        cbr_f = mem.alloc([64, 512], F32, "cbr_f")
        cbr_t = mem.alloc([64, 512], F32, "cbr_t")
        dhi = mem.alloc([128, 4], BF16, "dhi")
        dlo = mem.alloc([128, 4], F32, "dlo")
        cb_bf = [mem.alloc([128, 128], BF16, "cb_bf%d" % i) for i in range(2)]
        dec_bf = [mem.alloc([128, 512], BF16, "dec%d" % i) for i in range(2)]
        MT_bf = [mem.alloc([128, 512], BF16, "MT%d" % i) for i in range(2)]
        Xdt = [mem.alloc([128, 256], BF16, "Xdt%d" % i) for i in range(2)]
        Xdd = [mem.alloc([128, 256], BF16, "Xdd%d" % i) for i in range(2)]
        t1 = [mem.alloc([128, 256], F32, "t1%d" % i) for i in range(2)]
        t2 = [mem.alloc([128, 256], F32, "t2%d" % i) for i in range(2)]
        carry = mem.alloc([128, 256], F32, "carry")
        prev_bf = mem.alloc([128, 256], BF16, "prev_bf")
        yn_bf = [mem.alloc([128, 256], BF16, "yn%d" % i) for i in range(2)]
        ysm = [mem.alloc([128, 2, 128], BF16, "ysm%d" % i) for i in range(2)]
        ss = mem.alloc([128, 16], F32, "ss")
        rstd = mem.alloc([128, 16], F32, "rstd")
        junk = mem.alloc([128, 256], BF16, "junk")
        b_uT = [Buf("uT%d" % i) for i in range(2)]; b_dg = Buf("dg")
        b_dI = [Buf("dI%d" % i) for i in range(2)]; b_sel = [Buf("sel%d" % i) for i in range(2)]
        b_BT = [[Buf("BT%d_%d" % (i, q)) for q in range(4)] for i in range(2)]
        b_CT = [[Buf("CT%d_%d" % (i, q)) for q in range(4)] for i in range(2)]
        b_Btok = [Buf("Btok%d" % i) for i in range(2)]; b_xs = [Buf("xs%d" % i) for i in range(2)]
        b_gzt = [Buf("gzt%d" % i) for i in range(2)]; b_normw = [Buf("normw%d" % i) for i in range(3)]
        b_cbst = [Buf("cbst%d" % i) for i in range(2)]
        b_uall = [Buf("uall%d" % i) for i in range(16)]; b_tr = [Buf("tr%d" % i) for i in range(2)]; b_cbr = Buf("cbr"); b_dhl = Buf("dhl")
        b_cb = [Buf("cb%d" % i) for i in range(2)]
        b_dec = [Buf("dec%d" % i) for i in range(2)]; b_MT = [Buf("MT%d" % i) for i in range(2)]
        b_Xdt = [Buf("Xdt%d" % i) for i in range(2)]; b_Xdd = [Buf("Xdd%d" % i) for i in range(2)]
        b_t1 = [Buf("t1%d" % i) for i in range(2)]; b_t2 = [Buf("t2%d" % i) for i in range(2)]
        b_carry = Buf("carry"); b_prev = Buf("prev"); b_yn = [Buf("yn%d" % i) for i in range(2)]
        b_ysm = [Buf("ysm%d" % i) for i in range(2)]
        b_ss = Buf("ss"); b_rstd = Buf("rstd"); b_junk = Buf("junk")
        p.op("dve", lambda e: e.memset(uT[:, :, 0:3], 0.0), [], b_uT)
        for i in range(2):
            p.op("dve", lambda e, i=i: e.memset(cbst[i][:, :], 0.0), [], [b_cbst[i]])
        cbr0 = coffs["convbr"][0]
        sp0 = coffs["selS"][0]
        sn0 = coffs["selSneg"][0]

        def bc4(ap2d):
            return ap2d.unsqueeze(2).to_broadcast([128, 4, 64])

        def v4(ap2d):
            return ap2d.rearrange("p (a b) -> p a b", a=4)

        abank = [0]

        A_BANKS = (2, 3, 6, 7)

        def nextA():
            abank[0] += 1
            return A_BANKS[abank[0] % 4]

        def stageA(g):
            s = g % 2
            hs4 = slice(4 * g, 4 * g + 4)
            blks = [2 * g, 2 * g + 1, 16 + g, 24 + g]
            if g not in zslab:
                zslab[g] = load_slab("z%d" % g)
            slab, bs = zslab[g]
            p.dma("sp", [(normw[g % 3][:, :], cp("normw", g * 256, (g + 1) * 256))], "normw%d" % (g % 3),
                  writes=[b_normw[g % 3]])
            p.dma("sp", [(cbr_f[:, 0:256], cpack[0:64, cbr0 + g * 256:cbr0 + (g + 1) * 256]),
                         (cbr_f[:, 256:384], cpack[0:64, cbr0 + 2048 + g * 128:cbr0 + 2048 + (g + 1) * 128]),
                         (cbr_f[:, 384:512], cpack[0:64, cbr0 + 3072 + g * 128:cbr0 + 3072 + (g + 1) * 128])],
                  "cbr", writes=[b_cbr])
            p.dma("pool", [(selp[s][:, :, :].rearrange("p a b -> p (a b)"), cpack[:, sp0 + g * 512:sp0 + (g + 1) * 512]),
                           (seln[s][:, :, :].rearrange("p a b -> p (a b)"), cpack[:, sn0 + g * 512:sn0 + (g + 1) * 512])],
                  "sel%d" % s, writes=[b_sel[s]])
            CPY("dve", cbst[s][0:64, :], cbr_f[:, :], [b_cbr], [b_cbst[s]])
            TT("dve", cbr_t[:, :], cbr_f[:, :], cbst[s][0:64, :], ALU.subtract, [b_cbr, b_cbst[s]], [b_cbr])
            CPY("dve", cbst[s][32:64, :], cbr_t[32:64, :], [b_cbr], [b_cbst[s]])
            for j in range(4):
                for k in range(4):
                    TS("dve", dg[:, j * 4 + k, :], ident_f[:, :], convw[:, blks[j] * 4 + k:blks[j] * 4 + k + 1], None,
                       ALU.mult, None, [b_const], [b_dg])
            CPY("dve", dhi[:, :], dskip[:, hs4], [b_const], [b_dhl])
            TT("dve", dlo[:, :], dskip[:, hs4], dhi[:, :], ALU.subtract, [b_const, b_dhl], [b_dhl])
            for hl in range(4):
                TS("dve", dI[s][:, hl * 2, :], ident_f[:, :], dhi[:, hl:hl + 1], None, ALU.mult, None,
                   [b_const, b_dhl], [b_dI[s]])
                TS("dve", dI[s][:, hl * 2 + 1, :], ident_f[:, :], dlo[:, hl:hl + 1], None, ALU.mult, None,
                   [b_const, b_dhl], [b_dI[s]])
            yield
            nslab = load_slab("xs%d" % g)
            for c2 in range(8):
                bk = nextA()
                for cc in range(2):
                    c = c2 * 2 + cc
                    for kc in range(16):
                        MM(banks[bk][:, cc * 256:(cc + 1) * 256], xT_bf[:, kc, c * 128:(c + 1) * 128],
                           slab[:, kc, 0:256], kc == 0, kc == 15, [bs, b_xT[kc]], [bkb[bk]],
                           signal=(kc == 15 and cc == 1))
                SILU2(gz_tok[s][:, 2 * c2:2 * c2 + 2, :].rearrange("p a b -> p (a b)"), banks[bk][:, :],
                      [bkb[bk]], [b_gzt[s]])
                yield
            for j in range(4):
                sl = j % 2
                if j == 0:
                    slab, bs = nslab
                    nslab = load_slab("B%d" % g)
                    j0 = 0
                elif j == 1:
                    j0 = 128
                elif j == 2:
                    slab, bs = nslab
                    nslab = load_slab("C%d" % g)
                    j0 = 0
                else:
                    slab, bs = nslab
                    j0 = 0
                    if g < 7:
                        zslab[g + 1] = load_slab("z%d" % (g + 1))
                for g4 in range(4):
                    bk = nextA()
                    for kc in range(16):
                        MM(banks[bk][:, :], slab[:, kc, j0:j0 + 128], xT_bf[:, kc, g4 * 512:(g4 + 1) * 512],
                           kc == 0, kc == 15, [bs, b_xT[kc]], [bkb[bk]])
                    CPY("dve" if g4 % 2 else "act", uT[:, sl, 3 + g4 * 512:3 + (g4 + 1) * 512], banks[bk][:, :],
                        [bkb[bk]], [b_uT[sl]])
                    yield
                if j < 3:
                    for c4 in range(4):
                        bk = nextA()
                        for cc in range(4):
                            c = c4 * 4 + cc
                            for k in range(4):
                                MM(banks[bk][:, cc * 128:(cc + 1) * 128], uT[:, sl, c * 128 + k:c * 128 + k + 128],
                                   dg[:, j * 4 + k, :], cc == 0 and k == 0, False, [b_dg, b_uT[sl]], [bkb[bk]],
                                   signal=False)
                            MM(banks[bk][:, cc * 128:(cc + 1) * 128], ones_bf[:, 0:128],
                               cbst[s][:, j * 128:(j + 1) * 128], False, cc == 3, [b_const, b_cbst[s]], [bkb[bk]],
                               signal=(cc == 3))
                        if j < 2:
                            SILU2(xs_tok[s][:, 4 * c4:4 * c4 + 4, j * 128:(j + 1) * 128], banks[bk][:, :],
                                  [bkb[bk]], [b_xs[s]],
                                  in_view=lambda a: a.rearrange("p (a b) -> p a b", a=4),
                                  tt_view=lambda a: a.rearrange("p (a b) -> p a b", a=4))
                        else:
                            SILU2(B_tok[s][:, 4 * c4:4 * c4 + 4, :].rearrange("p a b -> p (a b)"), banks[bk][:, :],
                                  [bkb[bk]], [b_Btok[s]])
                        yield
                if j >= 2:
                    dst, bd = (BT[s], b_BT[s]) if j == 2 else (CT[s], b_CT[s])
                    for g4 in range(4):
                        bk = nextA()
                        for k in range(4):
                            MM(banks[bk][:, :], dg[:, j * 4 + k, :], uT[:, sl, g4 * 512 + k:g4 * 512 + k + 512],
                               k == 0, False, [b_dg, b_uT[sl]], [bkb[bk]], signal=False)
                        MM(banks[bk][:, :], cbst[s][:, j * 128:(j + 1) * 128], ones_bf[:, :], False, True,
                           [b_cbst[s], b_const], [bkb[bk]])
                        SILU2(dst[:, g4 * 512:(g4 + 1) * 512], banks[bk][:, :], [bkb[bk]], [bd[g4]])
                        yield

        def stageB_front(g, c):
            s = g % 2
            i2 = c % 2
            ck = slice(c * 128, (c + 1) * 128)
            hs4 = slice(4 * g, 4 * g + 4)
            cbk = 2 if g == 7 else 5
            MM(banks[cbk][:, 256:384], BT[s][:, ck], CT[s][:, ck], True, True, [b_BT[s][c // 4], b_CT[s][c // 4]], [bkb[cbk]])
            ACT(cb_bf[i2][:, :], banks[cbk][:, 256:384], AF.Copy, [bkb[cbk]], [b_cb[i2]], scale=0.25)
            sb = 0
            for hl in range(4):
                MM(banks[sb][:, hl * 128:(hl + 1) * 128], selp[s][:, hl, :], Ast[:, ck],
                   hl == 0, False, [b_sel[s], b_Ast], [bkb[sb]], signal=False)
                MM(banks[sb][:, hl * 128:(hl + 1) * 128], Ast[:, ck], seln[s][:, hl, :],
                   False, False, [b_sel[s], b_Ast], [bkb[sb]], signal=False)
            MM(banks[sb][:, :], ident_bf[:, :], mask4_bf[:, :], False, True, [b_const], [bkb[sb]])
            ACT(dec_bf[i2][:, :], banks[sb][:, :], AF.Exp, [bkb[sb]], [b_dec[i2]])
            TT("dve", MT_bf[i2][:, :].rearrange("p (a b) -> p a b", a=4),
               dec_bf[i2][:, :].rearrange("p (a b) -> p a b", a=4),
               cb_bf[i2][:, :].unsqueeze(1).to_broadcast([128, 4, 128]), ALU.mult,
               [b_dec[i2], b_cb[i2]], [b_MT[i2]])
            TT("pool", v4(Xdt[i2][:, :]), v4(xs_tok[s][:, c, :]), bc4(dt_tok[:, c, hs4]), ALU.mult,
               [b_xs[s], b_ssdc], [b_Xdt[i2]])
            TT("pool", v4(Xdd[i2][:, :]), v4(xs_tok[s][:, c, :]), bc4(dtde[:, c, hs4]), ALU.mult,
               [b_xs[s], b_ssdc], [b_Xdd[i2]])

        def stageB_back(g, c):
            s = g % 2
            i2 = c % 2
            yb = 1
            ck = slice(c * 128, (c + 1) * 128)
            hs4 = slice(4 * g, 4 * g + 4)
            for hl in range(4):
                MM(banks[yb][:, hl * 64:(hl + 1) * 64], MT_bf[i2][:, hl * 128:(hl + 1) * 128],
                   Xdt[i2][:, hl * 64:(hl + 1) * 64], hl == 0, False,
                   [b_MT[i2], b_Xdt[i2]], [bkb[yb]], signal=False)
            for hl in range(4):
                for e2 in range(2):
                    lastm = (hl == 3 and e2 == 1 and c == 0)
                    MM(banks[yb][:, hl * 64:(hl + 1) * 64], dI[s][:, hl * 2 + e2, :],
                       xs_tok[s][:, c, hl * 64:(hl + 1) * 64], False, lastm,
                       [b_dI[s], b_xs[s]], [bkb[yb]], signal=lastm)
            if c > 0:
                MM(banks[yb][:, 256:512], CT[s][:, ck], prev_bf[:, :], False, True, [b_CT[s][c // 4], b_prev],
                   [bkb[yb]])
            MM(banks[5][:, 0:256], B_tok[s][:, c, :], Xdd[i2][:, :], True, True, [b_Btok[s], b_Xdd[i2]], [bkb[5]])
            if c > 0:
                TT("dve", v4(t1[i2][:, :]), v4(banks[yb][:, 256:512]), bc4(ea_tok[:, c, hs4]), ALU.mult,
                   [bkb[yb], b_ssdc], [b_t1[i2]])
                TT("dve", t2[i2][:, :], t1[i2][:, :], banks[yb][:, 0:256], ALU.add,
                   [b_t1[i2], bkb[yb]], [b_t2[i2]])
            else:
                CPY("dve", t2[i2][:, :], banks[yb][:, 0:256], [bkb[yb]], [b_t2[i2]])
            TT("pool", u_all[:, c, :], t2[i2][:, :], gz_tok[s][:, c, :], ALU.mult, [b_t2[i2], b_gzt[s]], [b_uall[c]])
            ACT(junk[:, :], u_all[:, c, :], AF.Square, [b_uall[c]], [b_junk, b_ss], accum=ss[:, c:c + 1])
            if c < 15:
                if c == 0:
                    TS("dve", carry[:, :], banks[5][:, 0:256], 0.5, None, ALU.mult, None, [bkb[5]], [b_carry])
                else:
                    TT("dve", v4(carry[:, :]), v4(carry[:, :]), bc4(cd_bc[:, c, hs4]), ALU.mult,
                       [b_ssdc], [b_carry])
                    STT("dve", carry[:, :], banks[5][:, 0:256], 0.5, carry[:, :], ALU.mult, ALU.add,
                        [bkb[5]], [b_carry])
                CPY("act", prev_bf[:, :], carry[:, :], [b_carry], [b_prev])

        def stageC_prep(g, c0=0, c1=16):
            TS("dve", rstd[:, c0:c1], ss[:, c0:c1], 1.0 / 256.0, 4.0 * EPS, ALU.mult, ALU.add, [b_ss], [b_rstd])
            ACT(rstd[:, c0:c1], rstd[:, c0:c1], AF.Sqrt, [b_rstd], [b_rstd])
            p.op("dve", lambda e: e.reciprocal(out=rstd[:, c0:c1], in_=rstd[:, c0:c1]), [b_rstd], [b_rstd])

        def stageC_yn(g, c):
            i2 = c % 2
            STT("dve", yn_bf[i2][:, :], u_all[:, c, :], rstd[:, c:c + 1], normw[g % 3][:, :], ALU.mult, ALU.mult,
                [b_uall[c], b_rstd, b_normw[g % 3]], [b_yn[i2]])

        def stageC_tr(g, c):
            i2 = c % 2
            xbv = banks[4].bitcast(BF16)
            o0 = i2 * 256
            for j in range(2):
                TR(xbv[:, o0 + j * 128:o0 + (j + 1) * 128], yn_bf[i2][:, j * 128:(j + 1) * 128], ident_bf[:, :],
                   [b_yn[i2], b_const], [bkb[4]], signal=(j == 1))
            CPY("act" if c % 2 else "dve", ysm[i2][:, :, :],
                xbv[:, o0:o0 + 256].rearrange("p (a b) -> p a b", a=2), [bkb[4]], [b_ysm[i2]])
            c4_, cc_ = c // 4, c % 4
            mix_toks.append(p.dma("sp", [(mixs[c4_, :, 8 + 2 * g + j, cc_ * 128:(cc_ + 1) * 128], ysm[i2][:, j, :])
                                         for j in range(2)],
                                  "ysm%d" % i2, reads=[b_ysm[i2]]))

        for _ in stageA(0):
            pass
        for g in range(8):
            gen = stageA(g + 1) if g < 7 else None
            stageB_front(g, 0)
            for c in range(16):
                if g > 0:
                    stageC_tr(g - 1, c)
                if g > 0 and c + 1 < 16:
                    stageC_yn(g - 1, c + 1)
                if c + 1 < 16:
                    stageB_front(g, c + 1)
                stageB_back(g, c)
                if c == 15 and g < 7:
                    stageC_prep(g)
                    stageC_yn(g, 0)
                if g == 7 and phases >= 3:
                    p.dma("sp", [(w_bf[:, c, :], wob[:, c, :])], "w%d" % c, reads=[b_wob[c]], writes=[b_w[c]] + b_xT)
                if gen is not None:
                    for _ in range(3):
                        next(gen, None)
            if gen is not None:
                for _ in gen:
                    pass
    if phases >= 2:
        stageC_prep(7)
        for cc in range(16):
            stageC_yn(7, cc)
            stageC_tr(7, cc)
    p.barrier()
    mem.release(mem.base)

    out_toks = []
    if phases >= 3:
        mem.cur = (mem.base + 31) // 32 * 32 + 24 * D * 2
        lng = mem.alloc([128, D], F32, "lng"); lnb = mem.alloc([128, D], F32, "lnb")
        b_ln = Buf("ln")
        mb = [mem.alloc([128, 24, 512], BF16, "mb%d" % i) for i in range(2)]
        b_mb = [Buf("mb%d" % i) for i in range(2)]
        xt = [mem.alloc([128, D], F32, "xt%d" % i) for i in range(2)]
        rr = [mem.alloc([128, D], F32, "rr%d" % i) for i in range(2)]
        b_xt = [Buf("xt%d" % i) for i in range(2)]; b_rr = [Buf("rr%d" % i) for i in range(2)]
        st6 = mem.alloc([128, 4, 6], F32, "st6"); mv = mem.alloc([128, 2], F32, "mv")
        sc2 = mem.alloc([128, 2], F32, "sc2")
        b_st = Buf("st"); b_mv = Buf("mv"); b_sc2 = Buf("sc2")
        p.wait_all("sp", mix_toks)

        def load_mb(tb4):
            ms = tb4 % 2
            p.dma("sp", [(mb[ms][:, :, :].rearrange("p k t -> p (k t)"), mixs[tb4].rearrange("p k t -> p (k t)"))],
                  "mb%d" % ms, writes=[b_mb[ms]])

        def load_x(t):
            p.dma("sp", [(xt[t % 2][:, :], x_tok[t * 128:(t + 1) * 128, :])], "xt%d" % (t % 2), writes=[b_xt[t % 2]])

        load_mb(0)
        p.dma("sp", [(w_bf[:, 16:24, :], wob[:, 16:24, :])], "w16", reads=b_wob[16:24], writes=b_w[16:24])
        load_x(0)
        p.dma("sp", [(lng[:, :], cp("lng")), (lnb[:, :], cp("lnb"))], "ln", writes=[b_ln])
        load_mb(1)
        for tb4 in range(4):
            ms = tb4 % 2
            if tb4 >= 1 and tb4 + 1 < 4:
                load_mb(tb4 + 1)
            for tt in range(4):
                t = tb4 * 4 + tt
                xs_ = t % 2
                if t + 1 < 16:
                    load_x(t + 1)
                bset = (t % 2) * 4
                for kb in range(24):
                    for n in range(4):
                        MM(banks[bset + n][:, :], mb[ms][:, kb, tt * 128:(tt + 1) * 128],
                           w_bf[:, kb, n * 512:(n + 1) * 512], kb == 0, kb == 23, [b_mb[ms], b_w[kb]], [bkb[bset + n]])
                for n in range(4):
                    bk = bset + n
                    STT("dve", rr[xs_][:, n * 512:(n + 1) * 512], xt[xs_][:, n * 512:(n + 1) * 512], ALPHA,
                        banks[bk][:, :], ALU.mult, ALU.add, [b_xt[xs_], bkb[bk]], [b_rr[xs_]])
                    p.op("dve", lambda e, n=n, xs_=xs_: e.bn_stats(out=st6[:, n, :], in_=rr[xs_][:, n * 512:(n + 1) * 512]),
                         [b_rr[xs_]], [b_st])
                p.op("dve", lambda e: e.bn_aggr(out=mv[:, :], in_=st6[:, :, :].rearrange("p a b -> p (a b)")),
                     [b_st], [b_mv])
                TS("dve", sc2[:, 0:1], mv[:, 1:2], EPS, None, ALU.add, None, [b_mv], [b_sc2])
                ACT(sc2[:, 0:1], sc2[:, 0:1], AF.Sqrt, [b_sc2], [b_sc2])
                p.op("dve", lambda e: e.reciprocal(out=sc2[:, 0:1], in_=sc2[:, 0:1]), [b_sc2], [b_sc2])
                TS("dve", rr[xs_][:, :], rr[xs_][:, :], mv[:, 0:1], sc2[:, 0:1], ALU.subtract, ALU.mult,
                   [b_mv, b_sc2], [b_rr[xs_]])
                TT("pool", rr[xs_][:, :], rr[xs_][:, :], lng[:, :], ALU.mult, [b_ln], [b_rr[xs_]])
                TT("pool", rr[xs_][:, :], rr[xs_][:, :], lnb[:, :], ALU.add, [b_ln], [b_rr[xs_]])
                out_toks.append(p.dma("sp", [(out[t * 128:(t + 1) * 128, :], rr[xs_][:, :])], "rr%d" % xs_,
                                      reads=[b_rr[xs_]]))
    p.wait_all("sp", out_toks + mix_toks)
    p.emit()
    return nc, p, mem


_CACHE = {}


def kernel(x, w_in, b_forget, conv_w, conv_b, dt_bias, a_log, d_skip, ssd_norm_w, w_out, ln_g, ln_b):
    x = np.asarray(x, np.float32)
    wpack = make_wpack(np.asarray(w_in, np.float32)[0])
    cpack = make_cpack(np.asarray(b_forget, np.float32)[0], np.asarray(conv_w, np.float32)[0],
                       np.asarray(conv_b, np.float32)[0], np.asarray(dt_bias, np.float32)[0],
                       np.asarray(a_log, np.float32)[0], np.asarray(d_skip, np.float32)[0],
                       np.asarray(ssd_norm_w, np.float32)[0], np.asarray(ln_g, np.float32)[0],
                       np.asarray(ln_b, np.float32)[0])
    wo = np.ascontiguousarray(np.asarray(w_out, np.float32)[0])
    nc, _, _ = build_program()
    in_maps = []
    for b in range(NCORES):
        in_maps.append({"x_tok": np.ascontiguousarray(x[b]), "xT": np.ascontiguousarray(x[b].T),
                        "wpack": wpack, "w_out": wo, "cpack": cpack})
    res = run_bass_kernel_spmd(nc, in_maps, core_ids=list(range(NCORES)))
    return np.stack([np.asarray(r["out"], np.float32) for r in res.results], axis=0)
```

```python
import math
import numpy as np
import ml_dtypes
import concourse.bass as bass
import concourse.mybir as mybir
from concourse.bass_utils import run_bass_kernel_spmd

F32 = mybir.dt.float32
BF16 = mybir.dt.bfloat16
AF = mybir.ActivationFunctionType
ALU = mybir.AluOpType

L = 2048
D = 2048
NT = 16
NCORES = 8
SCALE = 1.0 / math.sqrt(128.0)
NEG = -30000.0
ALPHA = 2.0 ** 0.25
EPS = 1e-5

ENGS = ("pe", "act", "dve", "pool", "sp")


class Buf:
    __slots__ = ("name", "w", "r", "excl")

    def __init__(self, name, excl=False):
        self.name = name
        self.w = None
        self.r = {}
        self.excl = excl


class Prog:
    def __init__(self, nc):
        self.nc = nc
        self.ops = {e: [] for e in ENGS}
        self.cnt = {}
        self.known = {e: {} for e in ENGS}
        self.snap = {}
        self.sem_order = []
        self.nwaits = 0

    def _sem(self, key):
        if key not in self.cnt:
            self.cnt[key] = 0
            self.sem_order.append(key)

    def _collect(self, eng, reads, writes):
        waits = {}
        known = self.known[eng]

        def need(dep):
            if dep is None:
                return
            k, v = dep
            if eng == "pe" and k == "pe":
                return
            if known.get(k, 0) >= v:
                return
            if waits.get(k, 0) < v:
                waits[k] = v

        for b in reads:
            need(b.w)
        for b in writes:
            need(b.w)
            for d in b.r.items():
                need(d)
        for k, v in list(waits.items()):
            if known.get(k, 0) < v:
                known[k] = v
            s = self.snap.get((k, v))
            if s:
                for k2, v2 in s.items():
                    if known.get(k2, 0) < v2:
                        known[k2] = v2
        return {k: v for k, v in waits.items() if known.get(k, 0) <= v}

    def _mark(self, tok, reads, writes):
        for b in reads:
            if b.r.get(tok[0], 0) < tok[1]:
                b.r[tok[0]] = tok[1]
        for b in writes:
            b.w = tok
            b.r = {}

    def op(self, eng, fn, reads=(), writes=(), signal=True):
        self._sem(eng)
        ex = [b for b in reads if b.excl and b not in writes]
        if ex:
            writes = list(writes) + ex
        waits = self._collect(eng, reads, writes)
        if signal:
            self.cnt[eng] += 1
            tok = (eng, self.cnt[eng])
            self.snap[tok] = dict(self.known[eng])
        else:
            tok = (eng, self.cnt[eng] + 1)
        self._mark(tok, reads, writes)
        self.nwaits += len(waits)
        self.ops[eng].append((waits, fn, (eng, 1) if signal else None))
        return tok

    def dma(self, eng, pairs, key, reads=(), writes=(), **kw):
        key = "d:" + key
        self._sem(key)
        waits = self._collect(eng, reads, writes)
        for i, (o, a) in enumerate(pairs):
            def fn(e, o=o, a=a):
                return e.dma_start(out=o, in_=a, **kw)
            self.ops[eng].append((waits if i == 0 else {}, fn, (key, 16)))
        self.cnt[key] += 16 * len(pairs)
        tok = (key, self.cnt[key])
        self.snap[tok] = dict(self.known[eng])
        self._mark(tok, reads, writes)
        self.nwaits += len(waits)
        return tok

    def wait_all(self, eng, toks):
        waits = {}
        for k, v in toks:
            if self.known[eng].get(k, 0) < v and waits.get(k, 0) < v:
                waits[k] = v
        for k, v in waits.items():
            self.known[eng][k] = v
        self.ops[eng].append((waits, None, None))

    def barrier(self):
        toks = [(k, v) for k, v in self.cnt.items() if v > 0]
        for e in ENGS:
            self.wait_all(e, toks)

    def emit(self):
        nc = self.nc
        from contextlib import ExitStack
        with ExitStack() as st:
            sems = {}
            for k in self.sem_order:
                sems[k] = st.enter_context(nc.semaphore("s%d" % len(sems)))
            block = st.enter_context(nc.Block())

            def run(ename):
                def body(e):
                    for waits, fn, inc in self.ops[ename]:
                        for k, v in waits.items():
                            e.wait_ge(sems[k], v)
                        if fn is None:
                            continue
                        ins = fn(e)
                        if inc is not None:
                            ins.then_inc(sems[inc[0]], inc[1])
                return body

            block.tensor(run("pe"))
            block.scalar(run("act"))
            block.vector(run("dve"))
            block.gpsimd(run("pool"))
            block.sync(run("sp"))


class Mem:
    def __init__(self, nc, base, top):
        self.nc = nc
        self.base = base
        self.top = top
        self.cur = base
        self.n = 0
        self.peak = base

    def alloc(self, shape, dtype, name="t"):
        esz = 2 if dtype == BF16 else 4
        nbytes = int(np.prod(shape[1:])) * esz
        off = (self.cur + 31) // 32 * 32
        assert off + nbytes <= self.top, "SBUF overflow %s need %d at %d top %d" % (name, nbytes, off, self.top)
        self.cur = off + nbytes
        self.peak = max(self.peak, self.cur)
        self.n += 1
        return self.nc.alloc_sbuf_tensor_at("%s_%d" % (name, self.n), list(shape), dtype, offset=off)

    def mark(self):
        return self.cur

    def release(self, mark):
        self.cur = mark


Q0, K0, V0, F0, ZA0, ZS0, XBC0, DT0 = 0, 1024, 2048, 3072, 3080, 4104, 6152, 10248


def slab_plan():
    plan = []
    f = [-1] * 128
    dtc = [-1] * 128
    for r in (0, 32, 64):
        for i in range(8):
            f[r + i] = F0 + i
        for i in range(32):
            dtc[r + i] = DT0 + i
    plan.append(("f", f))
    plan.append(("dt", dtc))
    for hp in range(4):
        plan.append(("v%d" % hp, list(range(V0 + hp * 256, V0 + (hp + 1) * 256))))
        for h in (2 * hp, 2 * hp + 1):
            plan.append(("q%d" % h, list(range(Q0 + h * 128, Q0 + (h + 1) * 128))))
            plan.append(("k%d" % h, list(range(K0 + h * 128, K0 + (h + 1) * 128))))
            plan.append(("za%d" % h, list(range(ZA0 + h * 128, ZA0 + (h + 1) * 128))))
    for g in range(8):
        plan.append(("z%d" % g, list(range(ZS0 + g * 256, ZS0 + (g + 1) * 256))))
        plan.append(("xs%d" % g, list(range(XBC0 + g * 256, XBC0 + (g + 1) * 256))))
        plan.append(("B%d" % g, list(range(XBC0 + 2048 + g * 128, XBC0 + 2048 + (g + 1) * 128))))
        plan.append(("C%d" % g, list(range(XBC0 + 3072 + g * 128, XBC0 + 3072 + (g + 1) * 128))))
    offs = {}
    o = 0
    for name, cols in plan:
        offs[name] = (o, len(cols))
        o += 16 * len(cols)
    return plan, offs, o


CP_FIELDS = [("ident", 128), ("mask4", 512), ("ones", 512), ("selAneg", 1024), ("selS", 4096), ("selSneg", 4096),
             ("prm3", 3), ("dskip", 32), ("normw", 2048), ("lng", 2048),
             ("lnb", 2048), ("convw", 128), ("convbc", 32), ("convbr", 4096)]


def cp_layout():
    offs = {}
    o = 0
    for n, w in CP_FIELDS:
        offs[n] = (o, w)
        o += w
    return offs, o


def make_wpack(w_in):
    plan, offs, tot = slab_plan()
    wp = np.zeros((128, tot), np.float32)
    for name, cols in plan:
        o, W = offs[name]
        blk = np.zeros((D, W), np.float32)
        idx = np.array(cols)
        ok = idx >= 0
        blk[:, ok] = w_in[:, idx[ok]]
        wp[:, o:o + 16 * W] = blk.reshape(16, 128, W).transpose(1, 0, 2).reshape(128, 16 * W)
    return wp


def make_cpack(b_forget, conv_w, conv_b, dt_bias, a_log, d_skip, ssd_norm_w, ln_g, ln_b):
    offs, tot = cp_layout()
    cp = np.zeros((128, tot), np.float32)

    def put(name, arr):
        o, w = offs[name]
        cp[:, o:o + w] = arr

    put("ident", np.eye(128, dtype=np.float32))
    s = np.arange(128)[:, None]
    l = np.arange(128)[None, :]
    maskT = np.where(l >= s, 0.0, NEG).astype(np.float32)
    put("mask4", np.tile(maskT, (1, 4)))
    put("ones", np.ones((128, 512), np.float32))
    selA = np.zeros((128, 8, 128), np.float32)
    selS = np.zeros((128, 32, 128), np.float32)
    for k in range(96):
        selS[k, k % 32, :] = 1.0
        if k < 72 and (k % 32) < 8:
            selA[k, k % 32, :] = -1.0
    put("selAneg", selA.reshape(128, -1))
    put("selS", selS.reshape(128, -1))
    put("selSneg", -selS.reshape(128, -1))
    bf = np.zeros((128, 1), np.float32)
    dtb = np.zeros((128, 1), np.float32)
    alog = np.zeros((128, 1), np.float32)
    for r in (0, 32, 64):
        bf[r:r + 8, 0] = b_forget
        dtb[r:r + 32, 0] = dt_bias
        alog[r:r + 32, 0] = a_log
    put("prm3", np.concatenate([bf, dtb, alog], axis=1))
    put("dskip", np.tile(d_skip[None, :], (128, 1)))
    put("normw", np.tile(ssd_norm_w[None, :], (128, 1)))
    put("lng", np.tile(ln_g[None, :], (128, 1)))
    put("lnb", np.tile(ln_b[None, :], (128, 1)))
    put("convw", conv_w.T.reshape(32, 128, 4).transpose(1, 0, 2).reshape(128, 128))
    put("convbc", conv_b.reshape(32, 128).T)
    cbr = np.zeros((128, 4096), np.float32)
    cbr[0] = conv_b
    cbr[32] = conv_b
    put("convbr", cbr)
    return cp


def build_program(dbg=False, phases=3):
    nc = bass.Bass("TRN2", target_bir_lowering=False)
    plan, woffs, wtot = slab_plan()
    coffs, ctot = cp_layout()

    x_tok = nc.dram_tensor("x_tok", [L, D], F32, kind="ExternalInput").ap()
    xT = nc.dram_tensor("xT", [D, L], F32, kind="ExternalInput").ap()
    wpack = nc.dram_tensor("wpack", [128, wtot], F32, kind="ExternalInput").ap()
    w_out = nc.dram_tensor("w_out", [3072, D], F32, kind="ExternalInput").ap()
    cpack = nc.dram_tensor("cpack", [128, ctot], F32, kind="ExternalInput").ap()
    out = nc.dram_tensor("out", [L, D], F32, kind="ExternalOutput").ap()
    mixs = nc.dram_tensor("mixs", [4, 128, 24, 512], BF16, kind="ExternalOutput" if dbg else "Internal").ap()

    wob = nc.dram_tensor("wob", [128, 24, D], BF16, kind="Internal").ap()
    b_wob = [Buf("wob%d" % i) for i in range(24)]
    p = Prog(nc)
    mem = Mem(nc, nc.sbuf_base, nc.sbuf_top)
    banks = [nc.alloc_psum_tensor("bank%d" % i, [128, 512], F32) for i in range(8)]
    bkb = [Buf("bank%d" % i, excl=True) for i in range(8)]

    def cp(name, a=0, b=None):
        o, w = coffs[name]
        if b is None:
            b = w
        return cpack[:, o + a:o + b]

    def MM(outap, lhsT, rhs, start, stop, reads, writes, signal=None):
        if signal is None:
            signal = stop
        return p.op("pe", lambda e: e.matmul(outap, lhsT, rhs, start=start, stop=stop), reads, writes, signal)

    def TR(outap, in_, ident, reads, writes, signal=True):
        return p.op("pe", lambda e: e.transpose(outap, in_, ident), reads, writes, signal)

    def ACT(outap, in_, func, reads, writes, bias=None, scale=1.0, accum=None):
        def fn(e):
            kw = {}
            if bias is not None:
                kw["bias"] = bias
            if accum is not None:
                kw["accum_out"] = accum
            return e.activation(out=outap, in_=in_, func=func, scale=scale, **kw)
        return p.op("act", fn, reads, writes)

    def TT(eng, outap, in0, in1, op, reads, writes):
        return p.op(eng, lambda e: e.tensor_tensor(out=outap, in0=in0, in1=in1, op=op), reads, writes)

    def TS(eng, outap, in0, s1, s2, op0, op1, reads, writes):
        def fn(e):
            if s2 is None:
                return e.tensor_scalar(out=outap, in0=in0, scalar1=s1, scalar2=None, op0=op0)
            return e.tensor_scalar(out=outap, in0=in0, scalar1=s1, scalar2=s2, op0=op0, op1=op1)
        return p.op(eng, fn, reads, writes)

    def STT(eng, outap, in0, scalar, in1, op0, op1, reads, writes):
        return p.op(eng, lambda e: e.scalar_tensor_tensor(out=outap, in0=in0, scalar=scalar, in1=in1,
                                                          op0=op0, op1=op1), reads, writes)

    def CPY(eng, outap, in_, reads, writes):
        if eng == "act":
            return p.op("act", lambda e: e.copy(out=outap, in_=in_), reads, writes)
        return p.op(eng, lambda e: e.tensor_copy(out=outap, in_=in_), reads, writes)

    b_xT = [Buf("xT%d" % i) for i in range(16)]
    xT_bf = mem.alloc([128, 16, L], BF16, "xT_bf")
    w_bf = nc.alloc_sbuf_tensor_at("w_out_bf", [128, 24, D], BF16, offset=(mem.base + 31) // 32 * 32)
    b_w = [Buf("w%d" % i) for i in range(24)]
    w_v = w_out.rearrange("(kb p) n -> p kb n", p=128)
    ident_bf = mem.alloc([128, 128], BF16, "ident_bf")
    ident_f = mem.alloc([128, 128], F32, "ident_f")
    mask4_bf = mem.alloc([128, 512], BF16, "mask4")
    ones_bf = mem.alloc([128, 512], BF16, "ones_bf")
    ones_f = mem.alloc([128, 128], F32, "ones_f")
    prm = mem.alloc([128, 8], F32, "prm")
    dskip = mem.alloc([128, 32], F32, "dskip")
    convw = mem.alloc([128, 128], F32, "convw")
    convbc = mem.alloc([128, 32], F32, "convbc")
    b_const = Buf("const")
    b_prm = Buf("prm")
    slabs = [mem.alloc([128, 16, 256], BF16, "slab%d" % i) for i in range(2)]
    b_slab = [Buf("slab%d" % i) for i in range(2)]
    Ast = mem.alloc([128, L], BF16, "Ast")
    dt_tok = mem.alloc([128, 16, 32], F32, "dt_tok")
    ea_tok = mem.alloc([128, 16, 32], F32, "ea_tok")
    cd_bc = mem.alloc([128, 16, 32], F32, "cd_bc")
    dtde = mem.alloc([128, 16, 32], F32, "dtde")
    b_ssdc = Buf("ssdc")
    ttb = [mem.alloc([128, 512], F32, "tt%d" % i) for i in range(2)]
    b_tt = [Buf("tt%d" % i) for i in range(2)]
    tti = [0]

    def SILU2(outap, psum_ap, reads, writes, in_view=None, tt_view=None):
        k = tti[0] % 2
        tti[0] += 1
        ACT(ttb[k][:, :], psum_ap, AF.Tanh, reads, [b_tt[k]], scale=0.5)
        tv = ttb[k][:, :] if tt_view is None else tt_view(ttb[k][:, :])
        pv = psum_ap if in_view is None else in_view(psum_ap)
        STT("dve", outap, tv, 1.0, pv, ALU.add, ALU.mult, [b_tt[k]] + list(reads), writes)

    p.dma("pool", [(ident_bf[:, :], cp("ident")), (mask4_bf[:, :], cp("mask4")), (ones_bf[:, :], cp("ones")),
                   ], "constbf", writes=[b_const])
    p.dma("sp", [(ident_f[:, :], cp("ident")), (ones_f[:, :], cp("ones", 0, 128)), (prm[:, 0:3], cp("prm3")),
                 (dskip[:, :], cp("dskip")),
                 (convw[:, :], cp("convw")), (convbc[:, :], cp("convbc"))], "constf", writes=[b_const, b_prm])
    slab_i = [0]

    def load_slab(name):
        i = slab_i[0] % 2
        slab_i[0] += 1
        o, W = woffs[name]
        dst = slabs[i][:, :, 0:W] if W == 256 else slabs[i][:, :, 0:W]
        src = wpack[:, o:o + 16 * W].rearrange("p (kc w) -> p kc w", w=W)
        p.dma("pool", [(dst, src)], "slab%d" % i, writes=[b_slab[i]])
        return slabs[i], b_slab[i]

    slab_f, bs_f = load_slab("f")
    slab_dt, bs_dt = load_slab("dt")
    xTv = xT.rearrange("(kc p) l -> p kc l", p=128)
    for kc in range(16):
        p.dma("pool", [(xT_bf[:, kc, :], xTv[:, kc, :])], "xT%d" % kc, writes=[b_xT[kc]])

    def proj_fm(slab, bslab, j0, bank_ids, evac):
        for g4 in range(4):
            bk = bank_ids[g4 % len(bank_ids)]
            for kc in range(16):
                MM(banks[bk][:, :], slab[:, kc, j0:j0 + 128], xT_bf[:, kc, g4 * 512:(g4 + 1) * 512],
                   kc == 0, kc == 15, [bslab, b_xT[kc]], [bkb[bk]])
            evac(g4, bk)

    m0 = mem.mark()
    Fst = mem.alloc([128, L], BF16, "Fst")
    selA_bf = mem.alloc([128, 8, 128], BF16, "selA")
    G_tok = mem.alloc([128, 16, 8], F32, "G_tok")
    qT = [mem.alloc([128, L], BF16, "qT0")]
    kT = [mem.alloc([128, L], BF16, "kT0")]
    gz = [mem.alloc([128, L], F32, "gz0")]
    vp = [mem.alloc([128, 16, 256], BF16, "vp0")]
    b_q = [Buf("q%d" % i) for i in range(2)]; b_k = [Buf("k%d" % i) for i in range(2)]
    b_gz = [Buf("gz%d" % i) for i in range(2)]; b_vp = [Buf("vp%d" % i) for i in range(2)]
    p1_slabs = [mem.alloc([128, 16, 256], BF16, "xslab%d" % i) for i in range(2)] + slabs
    p1_bufs = [Buf("slab%d" % i) for i in (2, 3)] + b_slab
    p1_keys = ["slab2", "slab3", "slab0", "slab1"]
    p1_order = []
    for hp_ in range(4):
        p1_order.append("v%d" % hp_)
        for h_ in (2 * hp_, 2 * hp_ + 1):
            p1_order += ["q%d" % h_, "k%d" % h_, "za%d" % h_]
    p1_issued = {}
    p1_state = [0, 0]

    def p1_fill():
        while p1_state[0] < len(p1_order) and p1_state[0] - p1_state[1] < 4:
            i = p1_state[0]
            nm = p1_order[i]
            o, W = woffs[nm]
            src = wpack[:, o:o + 16 * W].rearrange("p (kc w) -> p kc w", w=W)
            p.dma("pool", [(p1_slabs[i % 4][:, :, 0:W], src)], p1_keys[i % 4], writes=[p1_bufs[i % 4]])
            p1_issued[nm] = (p1_slabs[i % 4], p1_bufs[i % 4])
            p1_state[0] += 1

    def p1_get(nm):
        assert p1_order[p1_state[1]] == nm
        if nm not in p1_issued:
            p1_fill()
        p1_state[1] += 1
        return p1_issued[nm]

    def proj_v(hp):
        slab, bs = p1_get("v%d" % hp)
        vs = hp % 2
        for t2 in range(8):
            bk = t2 % 2
            for tt in range(2):
                t = t2 * 2 + tt
                for kc in range(16):
                    MM(banks[bk][:, tt * 256:(tt + 1) * 256], xT_bf[:, kc, t * 128:(t + 1) * 128],
                       slab[:, kc, 0:256], kc == 0, kc == 15, [bs, b_xT[kc]], [bkb[bk]],
                       signal=(kc == 15 and tt == 1))
            eng = "act" if (t2 % 2 == 0 or hp == 0) else "dve"
            CPY(eng, vp[vs][:, 2 * t2:2 * t2 + 2, :].rearrange("p a b -> p (a b)"), banks[bk][:, :],
                [bkb[bk]], [b_vp[vs]])

    def proj_q(h):
        hs = h % 2
        slab, bs = p1_get("q%d" % h)
        proj_fm(slab, bs, 0, [0, 1], lambda g4, bk: ACT(
            qT[hs][:, g4 * 512:(g4 + 1) * 512], banks[bk][:, :], AF.Copy, [bkb[bk]], [b_q[hs]], scale=SCALE))

    def proj_k(h):
        hs = h % 2
        slab, bs = p1_get("k%d" % h)
        proj_fm(slab, bs, 0, [0, 1], lambda g4, bk: CPY(
            "dve", kT[hs][:, g4 * 512:(g4 + 1) * 512], banks[bk][:, :], [bkb[bk]], [b_k[hs]]))

    def proj_za(h):
        hs = h % 2
        slab, bs = p1_get("za%d" % h)
        proj_fm(slab, bs, 0, [0, 1], lambda g4, bk: SILU2(
            gz[hs][:, g4 * 512:(g4 + 1) * 512], banks[bk][:, :], [bkb[bk]], [b_gz[hs]]))

    m1 = mem.mark()
    Gf = mem.alloc([128, L], F32, "Gf")
    dtA = mem.alloc([128, L], F32, "dtA")
    r1 = mem.alloc([128, L], F32, "r1")
    tmpb = mem.alloc([128, L], BF16, "tmpb")
    b_G = Buf("G"); b_r1 = Buf("r1"); b_tmpb = Buf("tmpb")
    b_F = Buf("Fst"); b_Gtok = Buf("Gtok")
    p.dma("pool", [(selA_bf[:, :, :].rearrange("p a b -> p (a b)"), cp("selAneg"))], "selA", writes=[b_const])

    TS("dve", prm[:, 3:4], prm[:, 0:1], -1.0, None, ALU.mult, None, [b_const], [b_prm])
    ACT(prm[:, 5:6], prm[:, 2:3], AF.Exp, [b_prm], [b_prm])
    TS("dve", prm[:, 4:5], prm[:, 5:6], -1.0, None, ALU.mult, None, [b_prm], [b_prm])

    def split3(src, bsrc, dst, bdst):
        CPY("dve", dst[:, :], src[:, :], [bsrc], [bdst])
        TT("dve", r1[:, :], src[:, :], dst[:, :], ALU.subtract, [bsrc, bdst], [b_r1])
        CPY("dve", dst[32:64, :], r1[32:64, :], [b_r1], [bdst])
        CPY("dve", tmpb[64:96, :], r1[64:96, :], [b_r1], [b_tmpb])
        TT("dve", r1[64:96, :], r1[64:96, :], tmpb[64:96, :], ALU.subtract, [b_r1, b_tmpb], [b_r1])
        CPY("dve", dst[64:96, :], r1[64:96, :], [b_r1], [bdst])

    dtT = mem.alloc([128, L], F32, "dtT")
    acsT = mem.alloc([128, L], F32, "acsT")
    a_tok = mem.alloc([128, 16, 32], F32, "a_tok")
    dtA_tok = mem.alloc([128, 16, 32], F32, "dtA_tok")
    b_dtT = Buf("dtT"); b_acs = Buf("acs"); b_dtA = Buf("dtA"); b_atok = Buf("atok"); b_Ast = Buf("Ast")
    fl = lambda t: t[:, :, :].rearrange("p a b -> p (a b)")
    for kc in range(16):
        for u in range(8):
            sl_, bs_ = (slab_f, bs_f) if u < 4 else (slab_dt, bs_dt)
            g4 = u % 4
            MM(banks[u][:, :], sl_[:, kc, 0:128], xT_bf[:, kc, g4 * 512:(g4 + 1) * 512],
               kc == 0, kc == 15, [bs_, b_xT[kc]], [bkb[u]])
    for g4 in range(4):
        ACT(Gf[:, g4 * 512:(g4 + 1) * 512], banks[g4][:, :], AF.Exp, [bkb[g4], b_prm], [b_G],
            bias=prm[:, 3:4], scale=-1.0)
    for g4 in range(4):
        ACT(dtT[:, g4 * 512:(g4 + 1) * 512], banks[4 + g4][:, :], AF.Exp, [bkb[4 + g4], b_prm], [b_dtT],
            bias=prm[:, 1:2])
    p1_fill()
    ACT(Gf[:, :], Gf[:, :], AF.Ln, [b_G], [b_G], bias=1.0)
    p.op("dve", lambda e: e.tensor_tensor_scan(out=Gf[:, :], data0=ones_f[:, 0:1].to_broadcast([128, L]),
                                               data1=Gf[:, :], initial=0.0, op0=ALU.mult, op1=ALU.add),
         [b_G, b_const], [b_G])
    split3(Gf, b_G, Fst, b_F)
    ACT(dtT[:, :], dtT[:, :], AF.Ln, [b_dtT], [b_dtT], bias=1.0)
    TS("dve", dtA[:, :], dtT[:, :], prm[:, 4:5], None, ALU.mult, None, [b_dtT, b_prm], [b_dtA])
    for c in range(16):
        p.op("dve", lambda e, c=c: e.tensor_tensor_scan(
            out=acsT[:, c * 128:(c + 1) * 128], data0=ones_f[:, 0:1].to_broadcast([128, 128]),
            data1=dtA[:, c * 128:(c + 1) * 128], initial=0.0, op0=ALU.mult, op1=ALU.add), [b_dtA, b_const], [b_acs])
    split3(acsT, b_acs, Ast, b_Ast)
    if phases >= 1:
        proj_v(0)
        proj_q(0)
    for t in range(16):
        TR(banks[2][:, t * 8:(t + 1) * 8], Gf[0:8, t * 128:(t + 1) * 128], ident_f[0:8, 0:8],
           [b_G, b_const], [bkb[2]], signal=(t == 15))
    for (src, bsrc, bk) in ((dtT, b_dtT, 3), (acsT, b_acs, 4), (dtA, b_dtA, 5)):
        for c in range(16):
            TR(banks[bk][:, c * 32:(c + 1) * 32], src[0:32, c * 128:(c + 1) * 128], ident_f[0:32, 0:32],
               [bsrc, b_const], [bkb[bk]], signal=(c == 15))
    CPY("dve", G_tok[:, :, :].rearrange("p a b -> p (a b)"), banks[2][:, 0:128], [bkb[2]], [b_Gtok])
    CPY("dve", fl(dt_tok), banks[3][:, :], [bkb[3]], [b_ssdc])
    CPY("dve", fl(a_tok), banks[4][:, :], [bkb[4]], [b_atok])
    CPY("dve", fl(dtA_tok), banks[5][:, :], [bkb[5]], [b_atok])
    for c in range(16):
        MM(banks[6][:, c * 32:(c + 1) * 32], ones_f[:, :], dtA_tok[:, c, :], c == 0, c == 15,
           [b_atok, b_const], [bkb[6]])
    ACT(fl(cd_bc), banks[6][:, :], AF.Exp, [bkb[6]], [b_ssdc])
    ACT(fl(ea_tok), fl(a_tok), AF.Exp, [b_atok], [b_ssdc])
    TT("dve", fl(a_tok), banks[6][:, :], fl(a_tok), ALU.subtract, [bkb[6], b_atok], [b_atok])
    ACT(fl(dtde), fl(a_tok), AF.Exp, [b_atok], [b_ssdc])
    TT("dve", fl(dtde), fl(dtde), fl(dt_tok), ALU.mult, [b_ssdc], [b_ssdc])
    for tbl in (dt_tok, dtde, ea_tok):
        TS("dve", fl(tbl), fl(tbl), 0.5, None, ALU.mult, None, [b_ssdc], [b_ssdc])
    TS("dve", dskip[:, :], dskip[:, :], 0.5, None, ALU.mult, None, [b_const], [b_const])
    if phases >= 1:
        proj_k(0)
        proj_za(0)
    p.barrier()
    mem.release(m1)

    mix_toks = []
    zslab = {}
    if phases >= 1:
        qT.append(mem.alloc([128, L], BF16, "qT1"))
        kT.append(mem.alloc([128, L], BF16, "kT1"))
        gz.append(mem.alloc([128, L], F32, "gz1"))
        vp.append(mem.alloc([128, 16, 256], BF16, "vp1"))
        Pb = [mem.alloc([128, 512], BF16, "P%d" % i) for i in range(3)]
        rinv = [mem.alloc([128, 512], F32, "rinv%d" % i) for i in range(2)]
        tO = [mem.alloc([128, 512], F32, "tO%d" % i) for i in range(2)]
        attst = [mem.alloc([128, L], BF16, "attst%d" % i) for i in range(2)]
        b_P = [Buf("P%d" % i) for i in range(3)]; b_rinv = [Buf("rinv%d" % i) for i in range(2)]
        b_tO = [Buf("tO%d" % i) for i in range(2)]; b_att = [Buf("att%d" % i) for i in range(2)]
        for h in range(8):
            hs = h % 2
            vs = (h // 2) % 2
            hl = h % 2
            if h > 0:
                if h % 2 == 0:
                    proj_v(h // 2)
                proj_q(h)
                proj_k(h)
                proj_za(h)
            if h == 7 and phases >= 2:
                zslab[0] = load_slab("z0")

            p1_fill()
            if phases >= 3:
                for kb in range(3 * h, 3 * h + 3):
                    p.dma("pool", [(wob[:, kb, :], w_v[:, kb, :])], "wob%d" % kb, writes=[b_wob[kb]])
            steps = [(g, kb) for g in range(4) for kb in range(4 * g + 4)]

            def S_mm(i):
                g, kb = steps[i]
                sb = 2 + (i % 2)
                j = kb - 4 * g
                c0 = j * 128 if j > 0 else 0
                n = 512 - c0
                diag = j >= 0
                q0 = g * 512 + c0
                MM(banks[sb][:, c0:512], kT[hs][:, kb * 128:(kb + 1) * 128], qT[hs][:, q0:q0 + n],
                   True, False, [b_k[hs], b_q[hs]], [bkb[sb]], signal=False)
                MM(banks[sb][:, c0:512], selA_bf[:, h, :], Fst[:, q0:q0 + n],
                   False, not diag, [b_const, b_F], [bkb[sb]], signal=not diag)
                if diag:
                    MM(banks[sb][:, c0:c0 + 128], ident_bf[:, :], mask4_bf[:, 0:128],
                       False, True, [b_const], [bkb[sb]], signal=True)

            def E_act(i):
                g, kb = steps[i]
                sb = 2 + (i % 2)
                j = kb - 4 * g
                c0 = j * 128 if j > 0 else 0
                ps = i % 3
                ACT(Pb[ps][:, c0:512], banks[sb][:, c0:512], AF.Exp, [bkb[sb], b_Gtok], [b_P[ps]],
                    bias=G_tok[:, kb, h:h + 1])

            def PV_mm(i):
                g, kb = steps[i]
                j = kb - 4 * g
                c0 = j * 128 if j > 0 else 0
                ps = i % 3
                ob = 4 + (g % 2)
                rb = 6 + (g % 2)
                last = kb == 4 * g + 3
                MM(banks[ob][:, c0:512], vp[vs][:, kb, hl * 128:(hl + 1) * 128], Pb[ps][:, c0:512],
                   kb == 0, last, [b_vp[vs], b_P[ps]], [bkb[ob]])
                MM(banks[rb][:, c0:512], ones_bf[:, 0:128], Pb[ps][:, c0:512],
                   kb == 0, last, [b_const, b_P[ps]], [bkb[rb]])
                if last:
                    gs = g % 2
                    p.op("dve", lambda e: e.reciprocal(out=rinv[gs][:, :], in_=banks[rb][:, :]),
                         [bkb[rb]], [b_rinv[gs]])
                    STT("dve", tO[gs][:, :], banks[ob][:, :], 0.5, rinv[gs][:, :], ALU.mult, ALU.mult,
                       [bkb[ob], b_rinv[gs]], [b_tO[gs]])
                    TT("pool", attst[hs][:, g * 512:(g + 1) * 512], tO[gs][:, :], gz[hs][:, g * 512:(g + 1) * 512],
                       ALU.mult, [b_tO[gs], b_gz[hs]], [b_att[hs]])

            S_mm(0)
            for i in range(len(steps)):
                E_act(i)
                if i + 1 < len(steps):
                    S_mm(i + 1)
                PV_mm(i)
            mix_toks.append(p.dma("sp", [(mixs[q4, :, h, :], attst[hs][:, q4 * 512:(q4 + 1) * 512]) for q4 in range(4)],
                                  "att%d" % hs, reads=[b_att[hs]]))
    p.barrier()
    mem.release(m0)

    if phases >= 2:
        uT = mem.alloc([128, 2, L + 3], BF16, "uT")
        dg = mem.alloc([128, 16, 128], BF16, "dg")
        dI = [mem.alloc([128, 8, 128], BF16, "dI%d" % i) for i in range(2)]
        selp = [mem.alloc([128, 4, 128], BF16, "selp%d" % i) for i in range(2)]
        seln = [mem.alloc([128, 4, 128], BF16, "seln%d" % i) for i in range(2)]
        BT = [mem.alloc([128, L], BF16, "BT%d" % i) for i in range(2)]
        CT = [mem.alloc([128, L], BF16, "CT%d" % i) for i in range(2)]
        B_tok = [mem.alloc([128, 16, 128], BF16, "B_tok%d" % i) for i in range(2)]
        xs_tok = [mem.alloc([128, 16, 256], BF16, "xs_tok%d" % i) for i in range(2)]
        gz_tok = [mem.alloc([128, 16, 256], BF16, "gz_tok%d" % i) for i in range(2)]
        normw = [mem.alloc([128, 256], F32, "normw%d" % i) for i in range(3)]
        cbst = [mem.alloc([128, 512], BF16, "cbst%d" % i) for i in range(2)]
        u_all = mem.alloc([128, 16, 256], BF16, "u_all")
        cbr_f = mem.alloc([64, 512], F32, "cbr_f")
        cbr_t = mem.alloc([64, 512], F32, "cbr_t")
        dhi = mem.alloc([128, 4], BF16, "dhi")
        dlo = mem.alloc([128, 4], F32, "dlo")
        cb_bf = [mem.alloc([128, 128], BF16, "cb_bf%d" % i) for i in range(2)]
        dec_bf = [mem.alloc([128, 512], BF16, "dec%d" % i) for i in range(2)]
        MT_bf = [mem.alloc([128, 512], BF16, "MT%d" % i) for i in range(2)]
        Xdt = [mem.alloc([128, 256], BF16, "Xdt%d" % i) for i in range(2)]
        Xdd = [mem.alloc([128, 256], BF16, "Xdd%d" % i) for i in range(2)]
        t1 = [mem.alloc([128, 256], F32, "t1%d" % i) for i in range(2)]
        t2 = [mem.alloc([128, 256], F32, "t2%d" % i) for i in range(2)]
        carry = mem.alloc([128, 256], F32, "carry")
        prev_bf = mem.alloc([128, 256], BF16, "prev_bf")
        yn_bf = [mem.alloc([128, 256], BF16, "yn%d" % i) for i in range(2)]
        ysm = [mem.alloc([128, 2, 128], BF16, "ysm%d" % i) for i in range(2)]
        ss = mem.alloc([128, 16], F32, "ss")
        rstd = mem.alloc([128, 16], F32, "rstd")
        junk = mem.alloc([128, 256], BF16, "junk")
        b_uT = [Buf("uT%d" % i) for i in range(2)]; b_dg = Buf("dg")
        b_dI = [Buf("dI%d" % i) for i in range(2)]; b_sel = [Buf("sel%d" % i) for i in range(2)]
        b_BT = [Buf("BT%d" % i) for i in range(2)]; b_CT = [Buf("CT%d" % i) for i in range(2)]
        b_Btok = [Buf("Btok%d" % i) for i in range(2)]; b_xs = [Buf("xs%d" % i) for i in range(2)]
        b_gzt = [Buf("gzt%d" % i) for i in range(2)]; b_normw = [Buf("normw%d" % i) for i in range(3)]
        b_cbst = [Buf("cbst%d" % i) for i in range(2)]
        b_uall = [Buf("uall%d" % i) for i in range(16)]; b_tr = [Buf("tr%d" % i) for i in range(2)]; b_cbr = Buf("cbr"); b_dhl = Buf("dhl")
        b_cb = [Buf("cb%d" % i) for i in range(2)]
        b_dec = [Buf("dec%d" % i) for i in range(2)]; b_MT = [Buf("MT%d" % i) for i in range(2)]
        b_Xdt = [Buf("Xdt%d" % i) for i in range(2)]; b_Xdd = [Buf("Xdd%d" % i) for i in range(2)]
        b_t1 = [Buf("t1%d" % i) for i in range(2)]; b_t2 = [Buf("t2%d" % i) for i in range(2)]
        b_carry = Buf("carry"); b_prev = Buf("prev"); b_yn = [Buf("yn%d" % i) for i in range(2)]
        b_ysm = [Buf("ysm%d" % i) for i in range(2)]
        b_ss = Buf("ss"); b_rstd = Buf("rstd"); b_junk = Buf("junk")
        p.op("dve", lambda e: e.memset(uT[:, :, 0:3], 0.0), [], b_uT)
        for i in range(2):
            p.op("dve", lambda e, i=i: e.memset(cbst[i][:, :], 0.0), [], [b_cbst[i]])
        cbr0 = coffs["convbr"][0]
        sp0 = coffs["selS"][0]
        sn0 = coffs["selSneg"][0]

        def bc4(ap2d):
            return ap2d.unsqueeze(2).to_broadcast([128, 4, 64])

        def v4(ap2d):
            return ap2d.rearrange("p (a b) -> p a b", a=4)

        abank = [0]

        A_BANKS = (2, 3, 6, 7)

        def nextA():
            abank[0] += 1
            return A_BANKS[abank[0] % 4]

        def stageA(g):
            s = g % 2
            hs4 = slice(4 * g, 4 * g + 4)
            blks = [2 * g, 2 * g + 1, 16 + g, 24 + g]
            if g not in zslab:
                zslab[g] = load_slab("z%d" % g)
            slab, bs = zslab[g]
            p.dma("sp", [(normw[g % 3][:, :], cp("normw", g * 256, (g + 1) * 256))], "normw%d" % (g % 3),
                  writes=[b_normw[g % 3]])
            p.dma("sp", [(cbr_f[:, 0:256], cpack[0:64, cbr0 + g * 256:cbr0 + (g + 1) * 256]),
                         (cbr_f[:, 256:384], cpack[0:64, cbr0 + 2048 + g * 128:cbr0 + 2048 + (g + 1) * 128]),
                         (cbr_f[:, 384:512], cpack[0:64, cbr0 + 3072 + g * 128:cbr0 + 3072 + (g + 1) * 128])],
                  "cbr", writes=[b_cbr])
            p.dma("pool", [(selp[s][:, :, :].rearrange("p a b -> p (a b)"), cpack[:, sp0 + g * 512:sp0 + (g + 1) * 512]),
                           (seln[s][:, :, :].rearrange("p a b -> p (a b)"), cpack[:, sn0 + g * 512:sn0 + (g + 1) * 512])],
                  "sel%d" % s, writes=[b_sel[s]])
            CPY("dve", cbst[s][0:64, :], cbr_f[:, :], [b_cbr], [b_cbst[s]])
            TT("dve", cbr_t[:, :], cbr_f[:, :], cbst[s][0:64, :], ALU.subtract, [b_cbr, b_cbst[s]], [b_cbr])
            CPY("dve", cbst[s][32:64, :], cbr_t[32:64, :], [b_cbr], [b_cbst[s]])
            for j in range(4):
                for k in range(4):
                    TS("dve", dg[:, j * 4 + k, :], ident_f[:, :], convw[:, blks[j] * 4 + k:blks[j] * 4 + k + 1], None,
                       ALU.mult, None, [b_const], [b_dg])
            CPY("dve", dhi[:, :], dskip[:, hs4], [b_const], [b_dhl])
            TT("dve", dlo[:, :], dskip[:, hs4], dhi[:, :], ALU.subtract, [b_const, b_dhl], [b_dhl])
            for hl in range(4):
                TS("dve", dI[s][:, hl * 2, :], ident_f[:, :], dhi[:, hl:hl + 1], None, ALU.mult, None,
                   [b_const, b_dhl], [b_dI[s]])
                TS("dve", dI[s][:, hl * 2 + 1, :], ident_f[:, :], dlo[:, hl:hl + 1], None, ALU.mult, None,
                   [b_const, b_dhl], [b_dI[s]])
            yield
            nslab = load_slab("xs%d" % g)
            for c2 in range(8):
                bk = nextA()
                for cc in range(2):
                    c = c2 * 2 + cc
                    for kc in range(16):
                        MM(banks[bk][:, cc * 256:(cc + 1) * 256], xT_bf[:, kc, c * 128:(c + 1) * 128],
                           slab[:, kc, 0:256], kc == 0, kc == 15, [bs, b_xT[kc]], [bkb[bk]],
                           signal=(kc == 15 and cc == 1))
                SILU2(gz_tok[s][:, 2 * c2:2 * c2 + 2, :].rearrange("p a b -> p (a b)"), banks[bk][:, :],
                      [bkb[bk]], [b_gzt[s]])
                yield
            for j in range(4):
                sl = j % 2
                if j == 0:
                    slab, bs = nslab
                    nslab = load_slab("B%d" % g)
                    j0 = 0
                elif j == 1:
                    j0 = 128
                elif j == 2:
                    slab, bs = nslab
                    nslab = load_slab("C%d" % g)
                    j0 = 0
                else:
                    slab, bs = nslab
                    j0 = 0
                    if g < 7:
                        zslab[g + 1] = load_slab("z%d" % (g + 1))
                for g4 in range(4):
                    bk = nextA()
                    for kc in range(16):
                        MM(banks[bk][:, :], slab[:, kc, j0:j0 + 128], xT_bf[:, kc, g4 * 512:(g4 + 1) * 512],
                           kc == 0, kc == 15, [bs, b_xT[kc]], [bkb[bk]])
                    CPY("dve" if g4 % 2 else "act", uT[:, sl, 3 + g4 * 512:3 + (g4 + 1) * 512], banks[bk][:, :],
                        [bkb[bk]], [b_uT[sl]])
                    yield
                if j < 3:
                    for c4 in range(4):
                        bk = nextA()
                        for cc in range(4):
                            c = c4 * 4 + cc
                            for k in range(4):
                                MM(banks[bk][:, cc * 128:(cc + 1) * 128], uT[:, sl, c * 128 + k:c * 128 + k + 128],
                                   dg[:, j * 4 + k, :], cc == 0 and k == 0, False, [b_dg, b_uT[sl]], [bkb[bk]],
                                   signal=False)
                            MM(banks[bk][:, cc * 128:(cc + 1) * 128], ones_bf[:, 0:128],
                               cbst[s][:, j * 128:(j + 1) * 128], False, cc == 3, [b_const, b_cbst[s]], [bkb[bk]],
                               signal=(cc == 3))
                        if j < 2:
                            SILU2(xs_tok[s][:, 4 * c4:4 * c4 + 4, j * 128:(j + 1) * 128], banks[bk][:, :],
                                  [bkb[bk]], [b_xs[s]],
                                  in_view=lambda a: a.rearrange("p (a b) -> p a b", a=4),
                                  tt_view=lambda a: a.rearrange("p (a b) -> p a b", a=4))
                        else:
                            SILU2(B_tok[s][:, 4 * c4:4 * c4 + 4, :].rearrange("p a b -> p (a b)"), banks[bk][:, :],
                                  [bkb[bk]], [b_Btok[s]])
                        yield
                if j >= 2:
                    dst, bd = (BT[s], b_BT[s]) if j == 2 else (CT[s], b_CT[s])
                    for g4 in range(4):
                        bk = nextA()
                        for k in range(4):
                            MM(banks[bk][:, :], dg[:, j * 4 + k, :], uT[:, sl, g4 * 512 + k:g4 * 512 + k + 512],
                               k == 0, False, [b_dg, b_uT[sl]], [bkb[bk]], signal=False)
                        MM(banks[bk][:, :], cbst[s][:, j * 128:(j + 1) * 128], ones_bf[:, :], False, True,
                           [b_cbst[s], b_const], [bkb[bk]])
                        SILU2(dst[:, g4 * 512:(g4 + 1) * 512], banks[bk][:, :], [bkb[bk]], [bd])
                        yield

        def stageB_front(g, c):
            s = g % 2
            i2 = c % 2
            ck = slice(c * 128, (c + 1) * 128)
            hs4 = slice(4 * g, 4 * g + 4)
            cbk = 2 if g == 7 else 5
            MM(banks[cbk][:, 256:384], BT[s][:, ck], CT[s][:, ck], True, True, [b_BT[s], b_CT[s]], [bkb[cbk]])
            ACT(cb_bf[i2][:, :], banks[cbk][:, 256:384], AF.Copy, [bkb[cbk]], [b_cb[i2]], scale=0.25)
            sb = 0
            for hl in range(4):
                MM(banks[sb][:, hl * 128:(hl + 1) * 128], selp[s][:, hl, :], Ast[:, ck],
                   hl == 0, False, [b_sel[s], b_Ast], [bkb[sb]], signal=False)
                MM(banks[sb][:, hl * 128:(hl + 1) * 128], Ast[:, ck], seln[s][:, hl, :],
                   False, False, [b_sel[s], b_Ast], [bkb[sb]], signal=False)
            MM(banks[sb][:, :], ident_bf[:, :], mask4_bf[:, :], False, True, [b_const], [bkb[sb]])
            ACT(dec_bf[i2][:, :], banks[sb][:, :], AF.Exp, [bkb[sb]], [b_dec[i2]])
            TT("dve", MT_bf[i2][:, :].rearrange("p (a b) -> p a b", a=4),
               dec_bf[i2][:, :].rearrange("p (a b) -> p a b", a=4),
               cb_bf[i2][:, :].unsqueeze(1).to_broadcast([128, 4, 128]), ALU.mult,
               [b_dec[i2], b_cb[i2]], [b_MT[i2]])
            TT("pool", v4(Xdt[i2][:, :]), v4(xs_tok[s][:, c, :]), bc4(dt_tok[:, c, hs4]), ALU.mult,
               [b_xs[s], b_ssdc], [b_Xdt[i2]])
            TT("pool", v4(Xdd[i2][:, :]), v4(xs_tok[s][:, c, :]), bc4(dtde[:, c, hs4]), ALU.mult,
               [b_xs[s], b_ssdc], [b_Xdd[i2]])

        def stageB_back(g, c):
            s = g % 2
            i2 = c % 2
            yb = 1
            ck = slice(c * 128, (c + 1) * 128)
            hs4 = slice(4 * g, 4 * g + 4)
            for hl in range(4):
                MM(banks[yb][:, hl * 64:(hl + 1) * 64], MT_bf[i2][:, hl * 128:(hl + 1) * 128],
                   Xdt[i2][:, hl * 64:(hl + 1) * 64], hl == 0, False,
                   [b_MT[i2], b_Xdt[i2]], [bkb[yb]], signal=False)
            for hl in range(4):
                for e2 in range(2):
                    lastm = (hl == 3 and e2 == 1 and c == 0)
                    MM(banks[yb][:, hl * 64:(hl + 1) * 64], dI[s][:, hl * 2 + e2, :],
                       xs_tok[s][:, c, hl * 64:(hl + 1) * 64], False, lastm,
                       [b_dI[s], b_xs[s]], [bkb[yb]], signal=lastm)
            if c > 0:
                MM(banks[yb][:, 256:512], CT[s][:, ck], prev_bf[:, :], False, True, [b_CT[s], b_prev], [bkb[yb]])
            MM(banks[5][:, 0:256], B_tok[s][:, c, :], Xdd[i2][:, :], True, True, [b_Btok[s], b_Xdd[i2]], [bkb[5]])
            if c > 0:
                TT("dve", v4(t1[i2][:, :]), v4(banks[yb][:, 256:512]), bc4(ea_tok[:, c, hs4]), ALU.mult,
                   [bkb[yb], b_ssdc], [b_t1[i2]])
                TT("dve", t2[i2][:, :], t1[i2][:, :], banks[yb][:, 0:256], ALU.add,
                   [b_t1[i2], bkb[yb]], [b_t2[i2]])
            else:
                CPY("dve", t2[i2][:, :], banks[yb][:, 0:256], [bkb[yb]], [b_t2[i2]])
            TT("pool", u_all[:, c, :], t2[i2][:, :], gz_tok[s][:, c, :], ALU.mult, [b_t2[i2], b_gzt[s]], [b_uall[c]])
            ACT(junk[:, :], u_all[:, c, :], AF.Square, [b_uall[c]], [b_junk, b_ss], accum=ss[:, c:c + 1])
            if c < 15:
                if c == 0:
                    TS("dve", carry[:, :], banks[5][:, 0:256], 0.5, None, ALU.mult, None, [bkb[5]], [b_carry])
                else:
                    TT("dve", v4(carry[:, :]), v4(carry[:, :]), bc4(cd_bc[:, c, hs4]), ALU.mult,
                       [b_ssdc], [b_carry])
                    STT("dve", carry[:, :], banks[5][:, 0:256], 0.5, carry[:, :], ALU.mult, ALU.add,
                        [bkb[5]], [b_carry])
                CPY("act", prev_bf[:, :], carry[:, :], [b_carry], [b_prev])

        def stageC_prep(g, c0=0, c1=16):
            TS("dve", rstd[:, c0:c1], ss[:, c0:c1], 1.0 / 256.0, 4.0 * EPS, ALU.mult, ALU.add, [b_ss], [b_rstd])
            ACT(rstd[:, c0:c1], rstd[:, c0:c1], AF.Sqrt, [b_rstd], [b_rstd])
            p.op("dve", lambda e: e.reciprocal(out=rstd[:, c0:c1], in_=rstd[:, c0:c1]), [b_rstd], [b_rstd])

        def stageC_yn(g, c):
            i2 = c % 2
            STT("dve", yn_bf[i2][:, :], u_all[:, c, :], rstd[:, c:c + 1], normw[g % 3][:, :], ALU.mult, ALU.mult,
                [b_uall[c], b_rstd, b_normw[g % 3]], [b_yn[i2]])

        def stageC_tr(g, c):
            i2 = c % 2
            xbv = banks[4].bitcast(BF16)
            o0 = i2 * 256
            for j in range(2):
                TR(xbv[:, o0 + j * 128:o0 + (j + 1) * 128], yn_bf[i2][:, j * 128:(j + 1) * 128], ident_bf[:, :],
                   [b_yn[i2], b_const], [bkb[4]], signal=(j == 1))
            CPY("act" if c % 2 else "dve", ysm[i2][:, :, :],
                xbv[:, o0:o0 + 256].rearrange("p (a b) -> p a b", a=2), [bkb[4]], [b_ysm[i2]])
            c4_, cc_ = c // 4, c % 4
            mix_toks.append(p.dma("sp", [(mixs[c4_, :, 8 + 2 * g + j, cc_ * 128:(cc_ + 1) * 128], ysm[i2][:, j, :])
                                         for j in range(2)],
                                  "ysm%d" % i2, reads=[b_ysm[i2]]))

        for _ in stageA(0):
            pass
        for g in range(8):
            gen = stageA(g + 1) if g < 7 else None
            if g > 0:
                stageC_prep(g - 1)
                stageC_yn(g - 1, 0)
            stageB_front(g, 0)
            for c in range(16):
                if g > 0:
                    stageC_tr(g - 1, c)
                if g > 0 and c + 1 < 16:
                    stageC_yn(g - 1, c + 1)
                if c + 1 < 16:
                    stageB_front(g, c + 1)
                stageB_back(g, c)
                if g == 7 and phases >= 3:
                    p.dma("sp", [(w_bf[:, c, :], wob[:, c, :])], "w%d" % c, reads=[b_wob[c]], writes=[b_w[c]] + b_xT)
                if gen is not None:
                    for _ in range(3):
                        next(gen, None)
            if gen is not None:
                for _ in gen:
                    pass
    if phases >= 2:
        stageC_prep(7)
        for cc in range(16):
            stageC_yn(7, cc)
            stageC_tr(7, cc)
    p.barrier()
    mem.release(mem.base)

    out_toks = []
    if phases >= 3:
        mem.cur = (mem.base + 31) // 32 * 32 + 24 * D * 2
        lng = mem.alloc([128, D], F32, "lng"); lnb = mem.alloc([128, D], F32, "lnb")
        b_ln = Buf("ln")
        mb = [mem.alloc([128, 24, 512], BF16, "mb%d" % i) for i in range(2)]
        b_mb = [Buf("mb%d" % i) for i in range(2)]
        xt = [mem.alloc([128, D], F32, "xt%d" % i) for i in range(2)]
        rr = [mem.alloc([128, D], F32, "rr%d" % i) for i in range(2)]
        b_xt = [Buf("xt%d" % i) for i in range(2)]; b_rr = [Buf("rr%d" % i) for i in range(2)]
        st6 = mem.alloc([128, 4, 6], F32, "st6"); mv = mem.alloc([128, 2], F32, "mv")
        sc2 = mem.alloc([128, 2], F32, "sc2")
        b_st = Buf("st"); b_mv = Buf("mv"); b_sc2 = Buf("sc2")
        p.wait_all("sp", mix_toks)

        def load_mb(tb4):
            ms = tb4 % 2
            p.dma("sp", [(mb[ms][:, :, :].rearrange("p k t -> p (k t)"), mixs[tb4].rearrange("p k t -> p (k t)"))],
                  "mb%d" % ms, writes=[b_mb[ms]])

        def load_x(t):
            p.dma("sp", [(xt[t % 2][:, :], x_tok[t * 128:(t + 1) * 128, :])], "xt%d" % (t % 2), writes=[b_xt[t % 2]])

        load_mb(0)
        p.dma("sp", [(w_bf[:, 16:24, :], wob[:, 16:24, :])], "w16", reads=b_wob[16:24], writes=b_w[16:24])
        load_x(0)
        p.dma("sp", [(lng[:, :], cp("lng")), (lnb[:, :], cp("lnb"))], "ln", writes=[b_ln])
        load_mb(1)
        for tb4 in range(4):
            ms = tb4 % 2
            if tb4 >= 1 and tb4 + 1 < 4:
                load_mb(tb4 + 1)
            for tt in range(4):
                t = tb4 * 4 + tt
                xs_ = t % 2
                if t + 1 < 16:
                    load_x(t + 1)
                bset = (t % 2) * 4
                for kb in range(24):
                    for n in range(4):
                        MM(banks[bset + n][:, :], mb[ms][:, kb, tt * 128:(tt + 1) * 128],
                           w_bf[:, kb, n * 512:(n + 1) * 512], kb == 0, kb == 23, [b_mb[ms], b_w[kb]], [bkb[bset + n]])
                for n in range(4):
                    bk = bset + n
                    STT("dve", rr[xs_][:, n * 512:(n + 1) * 512], xt[xs_][:, n * 512:(n + 1) * 512], ALPHA,
                        banks[bk][:, :], ALU.mult, ALU.add, [b_xt[xs_], bkb[bk]], [b_rr[xs_]])
                    p.op("dve", lambda e, n=n, xs_=xs_: e.bn_stats(out=st6[:, n, :], in_=rr[xs_][:, n * 512:(n + 1) * 512]),
                         [b_rr[xs_]], [b_st])
                p.op("dve", lambda e: e.bn_aggr(out=mv[:, :], in_=st6[:, :, :].rearrange("p a b -> p (a b)")),
                     [b_st], [b_mv])
                TS("dve", sc2[:, 0:1], mv[:, 1:2], EPS, None, ALU.add, None, [b_mv], [b_sc2])
                ACT(sc2[:, 0:1], sc2[:, 0:1], AF.Sqrt, [b_sc2], [b_sc2])
                p.op("dve", lambda e: e.reciprocal(out=sc2[:, 0:1], in_=sc2[:, 0:1]), [b_sc2], [b_sc2])
                TS("dve", rr[xs_][:, :], rr[xs_][:, :], mv[:, 0:1], sc2[:, 0:1], ALU.subtract, ALU.mult,
                   [b_mv, b_sc2], [b_rr[xs_]])
                TT("dve", rr[xs_][:, :], rr[xs_][:, :], lng[:, :], ALU.mult, [b_ln], [b_rr[xs_]])
                TT("pool", rr[xs_][:, :], rr[xs_][:, :], lnb[:, :], ALU.add, [b_ln], [b_rr[xs_]])
                out_toks.append(p.dma("sp", [(out[t * 128:(t + 1) * 128, :], rr[xs_][:, :])], "rr%d" % xs_,
                                      reads=[b_rr[xs_]]))
    p.wait_all("sp", out_toks + mix_toks)
    p.emit()
    return nc, p, mem


_CACHE = {}


def kernel(x, w_in, b_forget, conv_w, conv_b, dt_bias, a_log, d_skip, ssd_norm_w, w_out, ln_g, ln_b):
    x = np.asarray(x, np.float32)
    wpack = make_wpack(np.asarray(w_in, np.float32)[0])
    cpack = make_cpack(np.asarray(b_forget, np.float32)[0], np.asarray(conv_w, np.float32)[0],
                       np.asarray(conv_b, np.float32)[0], np.asarray(dt_bias, np.float32)[0],
                       np.asarray(a_log, np.float32)[0], np.asarray(d_skip, np.float32)[0],
                       np.asarray(ssd_norm_w, np.float32)[0], np.asarray(ln_g, np.float32)[0],
                       np.asarray(ln_b, np.float32)[0])
    wo = np.ascontiguousarray(np.asarray(w_out, np.float32)[0])
    nc, _, _ = build_program()
    in_maps = []
    for b in range(NCORES):
        in_maps.append({"x_tok": np.ascontiguousarray(x[b]), "xT": np.ascontiguousarray(x[b].T),
                        "wpack": wpack, "w_out": wo, "cpack": cpack})
    res = run_bass_kernel_spmd(nc, in_maps, core_ids=list(range(NCORES)))
    return np.stack([np.asarray(r["out"], np.float32) for r in res.results], axis=0)
```
